# Optimizing a Trainium2 kernel written in Bass

```python
import math
import jax, jax.numpy as jnp
from jax import lax
import numpy as np

D_MODEL = 1024
BATCH = 16
SEQ = 2048
DEPTH = 1

MIX_WIDTH = D_MODEL
GDN_HEAD_DIM = 128
GDN_WIDTH = MIX_WIDTH // 2
GDN_HEADS = GDN_WIDTH // GDN_HEAD_DIM
GDN_CONV = 4
GDN_CHUNK = 64
MOBA_HEAD_DIM = 128
MOBA_WIDTH = MIX_WIDTH - GDN_WIDTH
MOBA_HEADS = MOBA_WIDTH // MOBA_HEAD_DIM
MOBA_BLOCK = 256
MOBA_TOPK = 3
MOBA_Q_CHUNK = 16
ROPE_THETA = 500000.0
ROPE_DIMS = MOBA_HEAD_DIM // 4
D_FF = 4 * D_MODEL
PLE_DIM = 256
RMS_EPS = 1e-6
IN_PROJ = 4 * GDN_WIDTH + 2 * GDN_HEADS + 3 * MOBA_WIDTH
IN_SPLITS = (GDN_WIDTH, 2 * GDN_WIDTH, 3 * GDN_WIDTH, 4 * GDN_WIDTH,
             4 * GDN_WIDTH + GDN_HEADS, 4 * GDN_WIDTH + 2 * GDN_HEADS,
             4 * GDN_WIDTH + 2 * GDN_HEADS + MOBA_WIDTH,
             4 * GDN_WIDTH + 2 * GDN_HEADS + 2 * MOBA_WIDTH)

kernel_name = "hybrid_gdn_moba_parallel_block"


def rms_norm(x, w):
    xf = x.astype(jnp.float32)
    y = xf * lax.rsqrt(jnp.mean(xf * xf, axis=-1, keepdims=True) + RMS_EPS)
    return (y * w.astype(jnp.float32)).astype(x.dtype)


def l2_normalize(x):
    xf = x.astype(jnp.float32)
    return xf * lax.rsqrt(jnp.sum(xf * xf, axis=-1, keepdims=True) + RMS_EPS)


def to_heads(t, n_heads):
    b, s, _ = t.shape
    return t.reshape(b, s, n_heads, -1).transpose(0, 2, 1, 3)


def from_heads(t):
    b, h, s, d = t.shape
    return t.transpose(0, 2, 1, 3).reshape(b, s, h * d)


def causal_depthwise_conv(x, w):
    k, c = w.shape
    return lax.conv_general_dilated(
        x, w[:, None, :].astype(x.dtype), window_strides=(1,), padding=[(k - 1, 0)],
        dimension_numbers=('NWC', 'WIO', 'NWC'), feature_group_count=c)


def partial_rotary(x, positions):
    half = ROPE_DIMS // 2
    inv_freq = ROPE_THETA ** (-jnp.arange(half, dtype=jnp.float32) * (2.0 / ROPE_DIMS))
    ang = positions.astype(jnp.float32)[:, None] * inv_freq[None, :]
    cos, sin = jnp.cos(ang), jnp.sin(ang)
    xr = x[..., :ROPE_DIMS].astype(jnp.float32)
    x1, x2 = xr[..., :half], xr[..., half:]
    rot = jnp.concatenate([x1 * cos - x2 * sin, x2 * cos + x1 * sin], axis=-1).astype(x.dtype)
    return jnp.concatenate([rot, x[..., ROPE_DIMS:]], axis=-1)


def chunk_gated_delta_rule(q, k, v, g, beta):
    b, h, t, dk = q.shape
    dv = v.shape[-1]
    c = GDN_CHUNK
    n = t // c
    q = q * (dk ** -0.5)
    q, k, v = (a.reshape(b, h, n, c, a.shape[-1]) for a in (q, k, v))
    g = jnp.cumsum(g.reshape(b, h, n, c), axis=-1)
    beta = beta.reshape(b, h, n, c)[..., None]
    k_beta = k * beta
    causal = jnp.tril(jnp.ones((c, c), dtype=bool))
    strict = jnp.tril(jnp.ones((c, c), dtype=bool), -1)
    decay = jnp.exp(jnp.where(causal, g[..., :, None] - g[..., None, :], -jnp.inf))
    lower = jnp.where(strict, jnp.einsum('bhncd,bhnsd->bhncs', k_beta, k) * decay, 0.0)
    eye = jnp.eye(c, dtype=q.dtype)
    rhs = jnp.concatenate([v * beta, k_beta * jnp.exp(g)[..., None]], axis=-1)
    sol = lax.linalg.triangular_solve(eye + lower, rhs, left_side=True, lower=True)
    u, w = sol[..., :dv], sol[..., dv:]
    intra = jnp.where(causal, jnp.einsum('bhncd,bhnsd->bhncs', q, k) * decay, 0.0)

    def step(state, xs):
        q_i, k_i, u_i, w_i, g_i, a_i = xs
        v_new = u_i - jnp.einsum('bhcd,bhdv->bhcv', w_i, state)
        o_i = (jnp.einsum('bhcd,bhdv->bhcv', q_i * jnp.exp(g_i)[..., None], state)
               + jnp.einsum('bhcs,bhsv->bhcv', a_i, v_new))
        g_last = g_i[..., -1:]
        state = (state * jnp.exp(g_last)[..., None]
                 + jnp.einsum('bhcd,bhcv->bhdv', k_i * jnp.exp(g_last - g_i)[..., None], v_new))
        return state, o_i

    xs = tuple(jnp.moveaxis(a, 2, 0) for a in (q, k, u, w, g, intra))
    s0 = jnp.zeros((b, h, dk, dv), q.dtype)
    _, o = lax.scan(step, s0, xs)
    return jnp.moveaxis(o, 0, 2).reshape(b, h, t, dv)


def gated_deltanet_group(gq, gk, gv, gz, gb, ga, conv_w, a_log, dt_bias, norm_w):
    b, t, _ = gq.shape
    qkv = jax.nn.silu(causal_depthwise_conv(jnp.concatenate([gq, gk, gv], axis=-1), conv_w))
    q, k, v = jnp.split(qkv, 3, axis=-1)
    q = l2_normalize(to_heads(q, GDN_HEADS))
    k = l2_normalize(to_heads(k, GDN_HEADS))
    v = to_heads(v, GDN_HEADS).astype(jnp.float32)
    beta = jax.nn.sigmoid(gb.astype(jnp.float32)).transpose(0, 2, 1)
    g = (-jnp.exp(a_log.astype(jnp.float32))
         * jax.nn.softplus(ga.astype(jnp.float32) + dt_bias.astype(jnp.float32))).transpose(0, 2, 1)
    o = chunk_gated_delta_rule(q, k, v, g, beta).transpose(0, 2, 1, 3)
    z = gz.reshape(b, t, GDN_HEADS, GDN_HEAD_DIM).astype(jnp.float32)
    o = rms_norm(o, norm_w) * jax.nn.silu(z)
    return o.reshape(b, t, GDN_WIDTH).astype(gq.dtype)


def moba_attention(q, k, v):
    b, h, t, d = q.shape
    nb = -(-t // MOBA_BLOCK)
    t_pad = nb * MOBA_BLOCK
    pad = [(0, 0), (0, 0), (0, t_pad - t), (0, 0)]
    kp, vp = jnp.pad(k, pad), jnp.pad(v, pad)
    kb = kp.reshape(b, h, nb, MOBA_BLOCK, d)
    vb = vp.reshape(b, h, nb, MOBA_BLOCK, d)
    k_mean = jnp.mean(kb.astype(jnp.float32), axis=3)
    gate = jnp.einsum('bhtd,bhnd->bhtn', q.astype(jnp.float32), k_mean)
    q_blk = jnp.arange(t) // MOBA_BLOCK
    past = jnp.arange(nb)[None, :] < q_blk[:, None]
    gate = jnp.where(past, gate, -jnp.inf)
    topk = min(MOBA_TOPK, nb)
    _, sel = lax.top_k(gate, topk)
    sel_valid = sel < q_blk[None, None, :, None]
    scale = d ** -0.5
    bi = jnp.arange(b)[:, None, None, None]
    hi = jnp.arange(h)[None, :, None, None]

    def query_chunk(ci):
        t0 = ci * MOBA_Q_CHUNK
        q_c = lax.dynamic_slice_in_dim(q, t0, MOBA_Q_CHUNK, axis=2)
        sel_c = lax.dynamic_slice_in_dim(sel, t0, MOBA_Q_CHUNK, axis=2)
        val_c = lax.dynamic_slice_in_dim(sel_valid, t0, MOBA_Q_CHUNK, axis=2)
        k_sel = kb[bi, hi, sel_c]
        v_sel = vb[bi, hi, sel_c]
        s_sel = jnp.einsum('bhqd,bhqnkd->bhqnk', q_c, k_sel).astype(jnp.float32) * scale
        s_sel = jnp.where(val_c[..., None], s_sel, -jnp.inf).reshape(b, h, MOBA_Q_CHUNK, topk * MOBA_BLOCK)
        blk_start = (t0 // MOBA_BLOCK) * MOBA_BLOCK
        k_own = lax.dynamic_slice_in_dim(kp, blk_start, MOBA_BLOCK, axis=2)
        v_own = lax.dynamic_slice_in_dim(vp, blk_start, MOBA_BLOCK, axis=2)
        s_own = jnp.einsum('bhqd,bhkd->bhqk', q_c, k_own).astype(jnp.float32) * scale
        qpos = t0 + jnp.arange(MOBA_Q_CHUNK)
        kpos = blk_start + jnp.arange(MOBA_BLOCK)
        s_own = jnp.where(kpos[None, :] <= qpos[:, None], s_own, -jnp.inf)
        prob = jax.nn.softmax(jnp.concatenate([s_sel, s_own], axis=-1), axis=-1).astype(v.dtype)
        p_sel = prob[..., :topk * MOBA_BLOCK].reshape(b, h, MOBA_Q_CHUNK, topk, MOBA_BLOCK)
        p_own = prob[..., topk * MOBA_BLOCK:]
        return (jnp.einsum('bhqnk,bhqnkd->bhqd', p_sel, v_sel)
                + jnp.einsum('bhqk,bhkd->bhqd', p_own, v_own))

    out = lax.map(query_chunk, jnp.arange(t // MOBA_Q_CHUNK))
    return jnp.moveaxis(out, 0, 2).reshape(b, h, t, d)


def moba_group(mq, mk, mv, positions):
    q = partial_rotary(to_heads(mq, MOBA_HEADS), positions)
    k = partial_rotary(to_heads(mk, MOBA_HEADS), positions)
    v = to_heads(mv, MOBA_HEADS)
    return from_heads(moba_attention(q, k, v))


def setup_inputs(seed: int = 0) -> dict:
    key = jax.random.key(seed)
    ks = jax.random.split(key, 18)
    f32 = jnp.float32

    def nrm(k, shape, scale):
        return jax.random.normal(k, shape, f32) * scale

    dt = jnp.exp(jax.random.uniform(ks[5], (DEPTH, GDN_HEADS), f32, math.log(1e-3), math.log(1e-1)))
    return {
        "x": nrm(ks[0], (BATCH, SEQ, D_MODEL), 1.0),
        "p": nrm(ks[1], (DEPTH, BATCH, SEQ, PLE_DIM), 1.0),
        "w_in": nrm(ks[2], (DEPTH, D_MODEL, IN_PROJ), D_MODEL ** -0.5),
        "conv_w": nrm(ks[3], (DEPTH, GDN_CONV, 3 * GDN_WIDTH), GDN_CONV ** -0.5),
        "a_log": jnp.log(jax.random.uniform(ks[4], (DEPTH, GDN_HEADS), f32, 1.0, 16.0)),
        "dt_bias": dt + jnp.log(-jnp.expm1(-dt)),
        "gdn_norm_w": 1.0 + nrm(ks[6], (DEPTH, GDN_HEAD_DIM), 0.02),
        "w_out": nrm(ks[7], (DEPTH, MIX_WIDTH, D_MODEL), MIX_WIDTH ** -0.5),
        "attn_pre_norm": 1.0 + nrm(ks[8], (DEPTH, D_MODEL), 0.02),
        "attn_post_norm": 1.0 + nrm(ks[9], (DEPTH, D_MODEL), 0.02),
        "mlp_pre_norm": 1.0 + nrm(ks[10], (DEPTH, D_MODEL), 0.02),
        "mlp_post_norm": 1.0 + nrm(ks[11], (DEPTH, D_MODEL), 0.02),
        "w_up": nrm(ks[12], (DEPTH, D_MODEL, D_FF), D_MODEL ** -0.5),
        "w_down": nrm(ks[13], (DEPTH, D_FF, D_MODEL), D_FF ** -0.5),
        "w_ple": nrm(ks[14], (DEPTH, PLE_DIM, D_MODEL), PLE_DIM ** -0.5),
        "w_ple_gate": nrm(ks[15], (DEPTH, D_MODEL, D_MODEL), D_MODEL ** -0.5),
    }


def reference(x, p, w_in, conv_w, a_log, dt_bias, gdn_norm_w, w_out, attn_pre_norm,
              attn_post_norm, mlp_pre_norm, mlp_post_norm, w_up, w_down, w_ple, w_ple_gate):
    t = x.shape[1]
    positions = jnp.arange(t, dtype=jnp.int32)
    h = x
    for i in range(DEPTH):
        u = rms_norm(h, attn_pre_norm[i])
        proj = u @ w_in[i]
        gq, gk, gv, gz, gb, ga, mq, mk, mv = jnp.split(proj, IN_SPLITS, axis=-1)
        o_gdn = gated_deltanet_group(gq, gk, gv, gz, gb, ga, conv_w[i], a_log[i], dt_bias[i], gdn_norm_w[i])
        o_moba = moba_group(mq, mk, mv, positions)
        mix = jnp.concatenate([o_gdn, o_moba], axis=-1) @ w_out[i]
        h = h + rms_norm(mix, attn_post_norm[i])
        f = jnp.square(jax.nn.relu(rms_norm(h, mlp_pre_norm[i]) @ w_up[i])) @ w_down[i]
        h = h + rms_norm(f, mlp_post_norm[i])
        h = h + jax.nn.sigmoid(h @ w_ple_gate[i]) * (p[i] @ w_ple[i])
    return h
```

```python
import numpy as np
from contextlib import ExitStack
import concourse.bass as bass
import concourse.mybir as mybir
from concourse.bass_utils import run_bass_kernel_spmd

F32 = mybir.dt.float32
BF16 = mybir.dt.bfloat16
AF = mybir.ActivationFunctionType
ALU = mybir.AluOpType
AX = mybir.AxisListType

T = 2048
D = 1024
NT = T // 128
INP = 3592
DFF = 4096
EPS = 1e-6


class Buf:
    __slots__ = ("ap", "w", "r", "name", "excl")

    def __init__(self, ap, name="", inherits=(), excl=False):
        self.ap = ap
        self.name = name
        self.excl = excl
        self.w = {}
        self.r = {}
        for b in inherits:
            for k, v in b.w.items():
                self.w[k] = max(self.w.get(k, 0), v)
            for k, v in b.r.items():
                self.r[k] = max(self.r.get(k, 0), v)

    def __getitem__(self, key):
        return self.ap[key]


class Prog:
    ENG = ("pe", "act", "dve", "pool", "sp")

    def __init__(self, nc, stack, n_dma_sems=32):
        self.nc = nc
        self.lists = {e: [] for e in self.ENG}
        self.cnt = {e: 0 for e in self.ENG}
        self.semobj = {}
        for e in self.ENG:
            self.semobj[("eng", e)] = stack.enter_context(nc.semaphore("s_" + e))
        self.dma_sems = {"sp": [], "pool": []}
        for q, n in (("sp", n_dma_sems), ("pool", 16)):
            for i in range(n):
                key = ("dma" + q, i)
                self.semobj[key] = stack.enter_context(nc.semaphore("s_dma_%s%d" % (q, i)))
                self.dma_sems[q].append([key, 0])
        self.dma_rr = {"sp": 0, "pool": 0}
        self.seen = {e: {} for e in self.ENG}
        self.nops = 0
        self.nwaits = 0

    def _deps(self, eng, reads, writes):
        deps = {}
        me = ("eng", eng)
        for b in reads:
            for k, v in b.w.items():
                if deps.get(k, 0) < v:
                    deps[k] = v
            if b.excl:
                for k, v in b.r.items():
                    if k != me and deps.get(k, 0) < v:
                        deps[k] = v
        for b in writes:
            for k, v in b.w.items():
                if deps.get(k, 0) < v:
                    deps[k] = v
            for k, v in b.r.items():
                if deps.get(k, 0) < v:
                    deps[k] = v
        waits = []
        seen = self.seen[eng]
        for k, v in deps.items():
            if eng == "pe" and k == ("eng", "pe"):
                continue
            if seen.get(k, 0) >= v:
                continue
            seen[k] = v
            waits.append((k, v))
        return waits

    @staticmethod
    def _mark(reads, writes, key, val):
        for b in reads:
            if b.r.get(key, 0) < val:
                b.r[key] = val
        for b in writes:
            b.w = {key: val}
            b.r = {}

    def op(self, eng, fn, reads=(), writes=(), inc=True):
        waits = self._deps(eng, reads, writes)
        key = ("eng", eng)
        if inc:
            self.cnt[eng] += 1
            val = self.cnt[eng]
        else:
            val = self.cnt[eng] + 1
        self._mark(reads, writes, key, val)
        self.lists[eng].append((waits, fn, (key, 1) if inc else None))
        self.nops += 1
        self.nwaits += len(waits)

    def dma(self, eng, out, in_, reads=(), writes=(), **kw):
        pool_ = self.dma_sems[eng]
        slot = pool_[self.dma_rr[eng]]
        self.dma_rr[eng] = (self.dma_rr[eng] + 1) % len(pool_)
        key, uses = slot
        waits = self._deps(eng, reads, writes)
        if uses > 0 and self.seen[eng].get(key, 0) < 16 * uses:
            self.seen[eng][key] = 16 * uses
            waits.append((key, 16 * uses))
        slot[1] = uses + 1
        val = 16 * (uses + 1)
        self._mark(reads, writes, key, val)

        def fn(e, out=out, in_=in_, kw=kw):
            return e.dma_start(out=out, in_=in_, **kw)

        self.lists[eng].append((waits, fn, (key, 16)))
        self.nops += 1
        self.nwaits += len(waits)

    def wait_all(self, eng, bufs):
        waits = self._deps(eng, bufs, ())
        self.lists[eng].append((waits, None, None))

    def emit(self):
        nc = self.nc
        engmap = {"pe": "tensor", "act": "scalar", "dve": "vector",
                  "pool": "gpsimd", "sp": "sync"}
        semobj = self.semobj
        with nc.Block() as block:
            for e in self.ENG:
                lst = self.lists[e]

                def body(engine, lst=lst):
                    for waits, fn, inc in lst:
                        for k, v in waits:
                            engine.wait_ge(semobj[k], v)
                        if fn is None:
                            continue
                        ins = fn(engine)
                        if inc is not None:
                            ins.then_inc(semobj[inc[0]], inc[1])

                getattr(block, engmap[e])(body)


class Arena:
    def __init__(self, ap, nbytes):
        self.ap = ap
        self.nbytes = nbytes
        self.live = []

    def view(self, off, shape, dt):
        esz = 2 if dt == BF16 else 4
        n = 1
        for s in shape:
            n *= s
        size = n * esz
        assert off % 4 == 0 and size % 4 == 0, (off, size)
        assert off + size <= self.nbytes, ("arena overflow", off, size)
        v = self.ap[:, off // 4:(off + size) // 4]
        if dt != F32:
            v = v.bitcast(dt)
        if len(shape) == 2:
            v = v.rearrange("p (a b) -> p a b", a=shape[0])
        elif len(shape) == 3:
            v = v.rearrange("p (a b c) -> p a b c", a=shape[0], b=shape[1])
        return v, size

    def buf(self, name, off, shape, dt):
        v, size = self.view(off, shape, dt)
        end = off + size
        inh = [b for (o, e, b) in self.live if o < end and off < e]
        nb = Buf(v, name, inherits=inh)
        self.live = [(o, e, b) for (o, e, b) in self.live if not (off <= o and e <= end)]
        self.live.append((off, end, nb))
        return nb

    def track(self, off, size, b):
        self.live.append((off, off + size, b))

    def bufs(self, name, off, n, shape, dt):
        out = []
        esz = 2 if dt == BF16 else 4
        sz = esz
        for s in shape:
            sz *= s
        for i in range(n):
            out.append(self.buf("%s%d" % (name, i), off + i * sz, shape, dt))
        return out


C_ID, C_MI, C_MS, C_TU, C_BD, C_EA, C_EB, C_ON, C_COS, C_SIN = (
    0, 128, 256, 384, 512, 640, 768, 896, 1024, 1280)
NCONST = 1536


def make_consts():
    c = np.zeros((128, NCONST), np.float32)
    s = np.arange(128)[:, None]
    j = np.arange(128)[None, :]
    same = (s // 64) == (j // 64)
    c[:, C_ID:C_ID + 128] = (s == j)
    c[:, C_MI:C_MI + 128] = (s <= j) & same
    c[:, C_MS:C_MS + 128] = (s < j) & same
    c[:, C_TU:C_TU + 128] = (s <= j)
    c[:, C_BD:C_BD + 128] = same
    c[:, C_EA:C_EA + 128] = (s < 64)
    c[:, C_EB:C_EB + 128] = (s >= 64)
    c[:, C_ON:C_ON + 128] = 1.0
    inv_freq = 500000.0 ** (-np.arange(16, dtype=np.float64) * (2.0 / 32))
    pos = np.arange(T, dtype=np.float64)
    ang = pos[:, None] * inv_freq[None, :]
    cos = np.cos(ang).astype(np.float32).reshape(NT, 128, 16).transpose(1, 0, 2).reshape(128, NT * 16)
    sin = np.sin(ang).astype(np.float32).reshape(NT, 128, 16).transpose(1, 0, 2).reshape(128, NT * 16)
    c[:, C_COS:C_COS + 256] = cos
    c[:, C_SIN:C_SIN + 256] = sin
    return c


ARENA_BYTES = 204800
O_CONST = 0
O_U = 8192
O_G = 40960
O_S = 122880
O_X = 126976
O_T = 184320


def build(nseq=2, taps=(), stop_after=None):
    nc = bass.Bass("TRN2", target_bir_lowering=False)

    def din(name, shape, dt=F32):
        return nc.dram_tensor(name, list(shape), dt, kind="ExternalInput").ap()

    x_d = din("x", [nseq, T, D])
    p_d = din("p", [nseq, T, 256])
    w_in_d = din("w_in", [D, INP])
    conv_w_d = din("conv_w", [4, 1536])
    a_log_d = din("a_log", [4])
    dt_bias_d = din("dt_bias", [4])
    gnw_d = din("gdn_norm_w", [128])
    w_out_d = din("w_out", [D, D])
    n_pre_d = din("attn_pre_norm", [D])
    n_post_d = din("attn_post_norm", [D])
    m_pre_d = din("mlp_pre_norm", [D])
    m_post_d = din("mlp_post_norm", [D])
    w_up_d = din("w_up", [D, DFF])
    w_down_d = din("w_down", [DFF, D])
    w_ple_d = din("w_ple", [256, D])
    w_gate_d = din("w_ple_gate", [D, D])
    consts_d = din("consts", [128, NCONST])
    out_d = nc.dram_tensor("out", [nseq, T, D], F32, kind="ExternalOutput").ap()

    def dint(name, shape):
        return nc.dram_tensor(name, list(shape), BF16, kind="Internal").ap()

    wb_in = dint("wb_in", [D, INP])
    wb_out = dint("wb_out", [2, 128, 8, 512])
    wb_up = dint("wb_up", [8, 128, 8, 512])
    wb_down = dint("wb_down", [4, 128, 32, 256])
    wb_gate = dint("wb_gate", [2, 128, 8, 512])
    wb_ple = dint("wb_ple", [128, 2, D])

    tap_out = {}

    with ExitStack() as st:
        P = Prog(nc, st)
        arena_t = st.enter_context(nc.sbuf_tensor("arena", [128, ARENA_BYTES // 4], F32))
        AR = Arena(arena_t[:, :], ARENA_BYTES)
        psum_t = st.enter_context(nc.psum_tensor("psum", [128, 4096], F32))
        PS = [Buf(psum_t[:, i * 512:(i + 1) * 512], "ps%d" % i, excl=True) for i in range(8)]

        def psbf(i):
            return PS[i].ap.bitcast(BF16)

        def ACT(out, in_, func, reads, writes, **kw):
            P.op("act", lambda e: e.activation(out=out, in_=in_, func=func, **kw), reads, writes)

        def TT(out, in0, in1, op, reads, writes, eng="dve"):
            P.op(eng, lambda e: e.tensor_tensor(out=out, in0=in0, in1=in1, op=op), reads, writes)

        def TS(out, in0, s1, s2, op0, op1, reads, writes, eng="dve"):
            if op1 is None:
                P.op(eng, lambda e: e.tensor_scalar(out=out, in0=in0, scalar1=s1, scalar2=None, op0=op0), reads, writes)
            else:
                P.op(eng, lambda e: e.tensor_scalar(out=out, in0=in0, scalar1=s1, scalar2=s2, op0=op0, op1=op1), reads, writes)

        def STT(out, in0, scalar, in1, op0, op1, reads, writes):
            P.op("dve", lambda e: e.scalar_tensor_tensor(out=out, in0=in0, scalar=scalar, in1=in1, op0=op0, op1=op1), reads, writes)

        def CP(eng, out, in_, reads, writes):
            if eng == "act":
                P.op("act", lambda e: e.copy(out=out, in_=in_), reads, writes)
            else:
                P.op(eng, lambda e: e.tensor_copy(out, in_), reads, writes)

        def MM(out, lhsT, rhs, start, stop, reads, writes, inc, sgc=False):
            if sgc:
                P.op("pe", lambda e: e.matmul(out, lhsT=lhsT, rhs=rhs, start=start, stop=stop, skip_group_check=True), reads, writes, inc=inc)
            else:
                P.op("pe", lambda e: e.matmul(out, lhsT=lhsT, rhs=rhs, start=start, stop=stop), reads, writes, inc=inc)

        def TR(out, in_, ident, reads, writes, inc):
            P.op("pe", lambda e: e.transpose(out, in_, ident), reads, writes, inc=inc)

        def bc_last(ap2, n):
            return ap2.unsqueeze(2).to_broadcast([ap2.shape[0], ap2.shape[1], n])

        def bc_mid(ap2, n):
            return ap2.unsqueeze(1).to_broadcast([ap2.shape[0], n, ap2.shape[1]])

        def tap(name, buf, ap=None, deps=()):
            if name not in taps:
                return
            ap = buf.ap if ap is None else ap
            shp = list(ap.shape)
            dt = ap.dtype
            d = nc.dram_tensor("tap_" + name, shp, dt, kind="ExternalOutput").ap()
            db = Buf(d, "tap_" + name)
            P.dma("sp", d, ap, reads=[buf] + list(deps), writes=[db])
            tap_out[name] = db

        cst = AR.buf("cst", O_CONST, [NCONST], F32)
        identb = AR.buf("identb", 6144, [128], BF16)
        onesb = AR.buf("onesb", 6400, [128], BF16)
        eps_t = AR.buf("eps", 6656, [1], F32)
        one_t = AR.buf("one", 6660, [1], F32)
        lnqs_t = AR.buf("lnqs", 6664, [1], F32)
        prmT = AR.buf("prmT", 6672, [65], F32)
        dtb_c = AR.buf("dtb", 6944, [4], F32)
        nA_c = AR.buf("nA", 6960, [4], F32)
        prm = AR.buf("prm", O_T, [128], F32)
        P.dma("sp", cst[:], consts_d, writes=[cst])
        ident_f = cst[:, C_ID:C_ID + 128]
        maskI = cst[:, C_MI:C_MI + 128]
        maskS = cst[:, C_MS:C_MS + 128]
        triU = cst[:, C_TU:C_TU + 128]
        bdones = cst[:, C_BD:C_BD + 128]
        Ea = cst[:, C_EA:C_EA + 128]
        Eb = cst[:, C_EB:C_EB + 128]
        ones_f = cst[:, C_ON:C_ON + 128]
        cos_t = cst[:, C_COS:C_COS + 256].rearrange("p (t f) -> p t f", t=NT)
        sin_t = cst[:, C_SIN:C_SIN + 256].rearrange("p (t f) -> p t f", t=NT)
        prm_parts = [Buf(prm[0:8, :], "prm0", inherits=[prm]), Buf(prm[8:16, :], "prm1", inherits=[prm]),
                     Buf(prm[16:64, :], "prm2", inherits=[prm]), Buf(prm[64:65, :], "prm3", inherits=[prm])]
        for b_ in prm_parts:
            AR.track(O_T, 512, b_)
        P.dma("sp", prm[0:8, :], n_pre_d.rearrange("(k p) -> k p", p=128), writes=[prm_parts[0]])
        P.dma("sp", prm[8:16, :], m_pre_d.rearrange("(k p) -> k p", p=128), writes=[prm_parts[1]])
        P.dma("sp", prm[16:64, :], conv_w_d.rearrange("j (c p) -> (j c) p", p=128), writes=[prm_parts[2]])
        P.dma("sp", prm[64:65, :], gnw_d.rearrange("(o p) -> o p", o=1), writes=[prm_parts[3]])
        P.dma("sp", dtb_c[:], dt_bias_d.partition_broadcast(128), writes=[dtb_c])
        P.dma("sp", nA_c[:], a_log_d.partition_broadcast(128), writes=[nA_c])
        P.op("dve", lambda e: e.tensor_copy(identb[:], ident_f), [cst], [identb])
        P.op("dve", lambda e: e.tensor_copy(onesb[:], ones_f), [cst], [onesb])
        P.op("dve", lambda e: e.memset(eps_t[:], EPS), (), [eps_t])
        P.op("dve", lambda e: e.memset(one_t[:], 1.0), (), [one_t])
        P.op("dve", lambda e: e.memset(lnqs_t[:], float(np.log(128.0 ** -0.5))), (), [lnqs_t])
        TR(PS[7][:, 0:65], prm[0:65, :], cst[0:65, C_ID:C_ID + 65], prm_parts + [cst], [PS[7]], True)
        CP("dve", prmT[:], PS[7][:, 0:65], [PS[7]], [prmT])
        wpre_c = Buf(prmT[:, 0:8], "wpre", inherits=[prmT])
        wmlp_c = Buf(prmT[:, 8:16], "wmlp", inherits=[prmT])
        cw_c = Buf(prmT[:, 16:64].rearrange("p (j c) -> p c j", j=4), "cw", inherits=[prmT])
        gnw_c = Buf(prmT[:, 64:65], "gnw", inherits=[prmT])

        def conv_w(dst, src, rows, rb, dep=()):
            bl = []
            for r0 in range(0, rows, rb):
                b = Buf(dst[r0:r0 + rb, :], "wb")
                P.dma("pool", dst[r0:r0 + rb, :], src[r0:r0 + rb, :], reads=list(dep), writes=[b])
                bl.append(b)
            return bl

        WB_IN = conv_w(wb_in, w_in_d, D, 256, [cst, prmT, dtb_c, nA_c])
        WB = {}

        def conv_t(dst, src, dep):
            b = Buf(dst, "wb")
            P.dma("pool", dst, src, reads=list(dep), writes=[b])
            return b

        def convert_rest(dep):
            WB["out"] = [conv_t(wb_out[hf], w_out_d[:, hf * 512:(hf + 1) * 512].rearrange("(k p) c -> p k c", p=128), dep) for hf in range(2)]
            WB["up"] = [conv_t(wb_up[cg], w_up_d[:, cg * 512:(cg + 1) * 512].rearrange("(k p) c -> p k c", p=128), dep) for cg in range(8)]
            WB["down"] = [conv_t(wb_down[q], w_down_d[:, q * 256:(q + 1) * 256].rearrange("(j p) c -> p j c", p=128), dep) for q in range(4)]
            WB["gate"] = [conv_t(wb_gate[hf], w_gate_d[:, hf * 512:(hf + 1) * 512].rearrange("(k p) c -> p k c", p=128), dep) for hf in range(2)]
            WB["ple"] = [conv_t(wb_ple, w_ple_d.rearrange("(k p) c -> p k c", p=128), dep)]

        def wview(wb, c0, ncols):
            return wb[:, c0:c0 + ncols].rearrange("(k p) c -> p k c", p=128)

        outbufs = []

        for seq in range(nseq):
            uT_all, _ = AR.view(O_U, [8, T], BF16)
            uT_reg = AR.buf("uTreg", O_U, [8, T], BF16)
            uT = [Buf(uT_all[:, :, t * 128:(t + 1) * 128], "uT%d" % t, inherits=[uT_reg]) for t in range(NT)]
            for b_ in uT:
                AR.track(O_U, 8 * T * 2, b_)
            xt = AR.bufs("xt", O_T, 3, [D], F32)
            xn = AR.bufs("xn", O_T + 12288, 2, [D], BF16)
            junk = AR.buf("junk", O_T + 16384, [D], BF16)
            ssA = AR.bufs("ssA", O_T + 18432, 3, [1], F32)

            def a_load(t):
                P.dma("sp", xt[t % 3][:], x_d[seq, t * 128:(t + 1) * 128, :], writes=[xt[t % 3]])

            def a_stage1(t):
                b3, b = t % 3, t % 2
                P.op("dve", lambda e: e.scalar_tensor_tensor(out=junk[:], in0=xt[b3][:], scalar=1.0, in1=xt[b3][:], op0=ALU.mult,
                                                                op1=ALU.mult, accum_out=ssA[b3][:]), [xt[b3]], [junk, ssA[b3]])
                ACT(ssA[b3][:], ssA[b3][:], AF.Sqrt, [ssA[b3], eps_t], [ssA[b3]], scale=1.0 / D, bias=eps_t[:])
                P.op("dve", lambda e: e.reciprocal(ssA[b3][:], ssA[b3][:]), [ssA[b3]], [ssA[b3]])
                ACT(xn[b][:], xt[b3][:], AF.Copy, [xt[b3], ssA[b3]], [xn[b]], scale=ssA[b3][:])

            def a_stage2(t):
                b, pb = t % 2, t % 2
                for k in range(8):
                    TR(psbf(pb)[:, k * 128:(k + 1) * 128], xn[b][:, k * 128:(k + 1) * 128], identb[:],
                       [xn[b], identb], [PS[pb]], inc=(k == 7))
                TT(uT[t][:], psbf(pb).rearrange("p (k c) -> p k c", k=8), bc_last(wpre_c[:], 128), ALU.mult,
                   [PS[pb], wpre_c], [uT[t]])

            a_load(0)
            a_load(1)
            for t in range(NT + 1):
                if t + 2 < NT:
                    a_load(t + 2)
                if t < NT:
                    a_stage1(t)
                if t >= 1:
                    a_stage2(t - 1)
            if seq == 0:
                convert_rest([uT[NT - 1]])
                tap("uT", uT_reg, uT_all, deps=uT)
            if stop_after == "A":
                break

            qT_all, _ = AR.view(O_G, [4, T], BF16)
            kT_all, _ = AR.view(O_G + 16384, [4, T], BF16)
            qT = AR.buf("qT", O_G, [4, T], BF16)
            kT = AR.buf("kT", O_G + 16384, [4, T], BF16)
            Vtok = AR.buf("Vtok", O_G + 32768, [NT, 512], BF16)
            Ktok = AR.buf("Ktok", O_G + 49152, [NT, 512], BF16)
            Zs = AR.buf("Zs", O_G + 65536, [NT, 512], BF16)
            beta = AR.buf("beta", O_S, [NT, 4], F32)
            gstep = AR.buf("gstep", O_S + 256, [NT, 4], F32)
            gcum = AR.buf("gcum", O_S + 512, [NT, 4], F32)
            eg = AR.buf("eg", O_S + 768, [NT, 4], F32)
            egl = AR.buf("egl", O_S + 1024, [NT, 4], F32)
            EGL = AR.buf("EGL", O_S + 1280, [2, NT, 4], F32)
            gbga = AR.buf("gbga", O_S + 1792, [NT, 8], F32)
            tmpg = AR.buf("tmpg", O_S + 2304, [NT, 4], F32)
            stage = AR.bufs("stage", O_X, 2, [2052], F32)
            acc_l = [AR.buf("acc", O_X + 16416, [T], F32), AR.buf("acc2", O_T, [T], F32)]
            sil_l = [AR.buf("sil", O_X + 24608, [T], F32), AR.buf("sil2", O_T + 8192, [T], F32)]
            sq_l = [AR.buf("sq", O_X + 32800, [T], BF16), AR.buf("sq2", O_T + 16384, [T], BF16)]
            wg = AR.bufs("wg", O_X + 36896, 2, [8, 512], BF16)
            wsm = AR.buf("wsm", O_X + 53280, [8, 8], BF16)
            for sgb in stage:
                P.op("dve", lambda e, sgb=sgb: e.memset(sgb[:, 0:4], 0.0), (), [sgb])

            P.dma("sp", wg[0][:], wview(wb_in, 1536, 512), reads=WB_IN, writes=[wg[0]])
            P.dma("sp", wsm[:], wview(wb_in, 2048, 8), reads=WB_IN, writes=[wsm])
            for t in range(NT):
                pb = t % 2
                for k in range(8):
                    MM(PS[pb][:, :], uT[t][:, k, :], wg[0][:, k, :], k == 0, k == 7, [uT[t], wg[0]], [PS[pb]], k == 7)
                ACT(Zs[:, t, :], PS[pb][:, :], AF.Silu, [PS[pb]], [Zs])
                for k in range(8):
                    MM(PS[2][:, t * 8:(t + 1) * 8], uT[t][:, k, :], wsm[:, k, :], k == 0, k == 7, [uT[t], wsm], [PS[2]],
                       (k == 7 and t == NT - 1))
            CP("dve", gbga[:], PS[2][:, 0:NT * 8].rearrange("p (t c) -> p t c", t=NT), [PS[2]], [gbga])
            if seq == 0:
                ACT(nA_c[:], nA_c[:], AF.Exp, [nA_c], [nA_c])
                TS(nA_c[:], nA_c[:], -1.0, None, ALU.mult, None, [nA_c], [nA_c])
            ACT(beta[:], gbga[:, :, 0:4], AF.Sigmoid, [gbga], [beta])
            TT(tmpg[:], gbga[:, :, 4:8], bc_mid(dtb_c[:], NT), ALU.add, [gbga, dtb_c], [tmpg])
            ACT(tmpg[:], tmpg[:], AF.Exp, [tmpg], [tmpg])
            ACT(tmpg[:], tmpg[:], AF.Ln, [tmpg, one_t], [tmpg], bias=one_t[:])
            TT(gstep[:], tmpg[:], bc_mid(nA_c[:], NT), ALU.mult, [tmpg, nA_c], [gstep])
            for i, m in enumerate((maskI, bdones, Ea, Eb)):
                MM(PS[3][:, i * 64:(i + 1) * 64], m, gstep[:].rearrange("p t c -> p (t c)"), True, True, [cst, gstep], [PS[3]], i == 3)
            CP("dve", gcum[:], PS[3][:, 0:64].rearrange("p (t c) -> p t c", t=NT), [PS[3]], [gcum])
            ACT(eg[:], gcum[:], AF.Exp, [gcum], [eg])
            TT(egl[:], PS[3][:, 64:128].rearrange("p (t c) -> p t c", t=NT), gcum[:], ALU.subtract, [PS[3], gcum], [egl])
            ACT(egl[:], egl[:], AF.Exp, [egl], [egl])
            ACT(EGL[:], PS[3][:, 128:256].rearrange("p (a t c) -> p a t c", a=2, t=NT), AF.Exp, [PS[3]], [EGL])
            if seq == 0:
                tap("beta", beta); tap("gcum", gcum); tap("Zs", Zs); tap("EGL", EGL); tap("egl", egl)

            def b_in(c):
                grp, h = c // 4, c % 4
                wgi = (grp + 1) % 2
                if h == 0:
                    P.dma("sp", wg[wgi][:], wview(wb_in, grp * 512, 512), reads=WB_IN, writes=[wg[wgi]])
                sg = stage[c % 2]
                for tg in range(4):
                    pb = tg % 2
                    for k in range(8):
                        MM(PS[pb][:, :], wg[wgi][:, k, h * 128:(h + 1) * 128], uT_all[:, k, tg * 512:(tg + 1) * 512],
                           k == 0, k == 7, [wg[wgi]] + uT[tg * 4:tg * 4 + 4], [PS[pb]], k == 7)
                    CP("act", sg[:, 4 + tg * 512: 4 + (tg + 1) * 512], PS[pb][:, :], [PS[pb]], [sg])

            def b_conv(c):
                sg = stage[c % 2]
                acc = acc_l[c % 2]
                TS(acc[:], sg[:, 4:4 + T], cw_c[:, c, 3:4], None, ALU.mult, None, [sg, cw_c], [acc])
                for j in range(3):
                    STT(acc[:], sg[:, 1 + j:1 + j + T], cw_c[:, c, j:j + 1], acc[:], ALU.mult, ALU.add, [sg, cw_c, acc], [acc])

            def b_silu(c):
                grp = c // 4
                acc, sil, sq = acc_l[c % 2], sil_l[c % 2], sq_l[c % 2]
                if grp == 2:
                    ACT(sq[:], acc[:], AF.Silu, [acc], [sq])
                else:
                    ACT(sil[:], acc[:], AF.Silu, [acc], [sil])
                    TT(sq[:], sil[:], sil[:], ALU.mult, [sil], [sq], eng="pool")

            def b_fin(c):
                grp, h = c // 4, c % 4
                acc, sil, sq = acc_l[c % 2], sil_l[c % 2], sq_l[c % 2]
                if grp == 2:
                    for half in range(2):
                        for tt in range(8):
                            t = half * 8 + tt
                            TR(psbf(4 + half)[:, tt * 128:(tt + 1) * 128], sq[:, t * 128:(t + 1) * 128], identb[:],
                               [sq, identb], [PS[4 + half]], tt == 7)
                        CP("act", Vtok[:, half * 8:(half + 1) * 8, h * 128:(h + 1) * 128],
                           psbf(4 + half).rearrange("p (t c) -> p t c", t=8), [PS[4 + half]], [Vtok])
                    return
                dstT = qT if grp == 0 else kT
                for tg in range(4):
                    MM(PS[2 + (tg % 2)][:, :], onesb[:], sq[:, tg * 512:(tg + 1) * 512], True, True,
                       [onesb, sq], [PS[2 + (tg % 2)]], True)
                    ACT(acc[:, tg * 512:(tg + 1) * 512], PS[2 + (tg % 2)][:, :], AF.Ln, [PS[2 + (tg % 2)], eps_t], [acc], bias=eps_t[:])
                if grp == 0:
                    ACT(acc[:], acc[:], AF.Exp, [acc, lnqs_t], [acc], scale=-0.5, bias=lnqs_t[:])
                else:
                    ACT(acc[:], acc[:], AF.Exp, [acc], [acc], scale=-0.5)
                for half in range(2):
                    hsl = slice(half * 1024, (half + 1) * 1024)
                    TT(dstT[:, h, hsl], sil[:, hsl], acc[:, hsl], ALU.mult, [sil, acc], [dstT])
                if grp == 1:
                    for half in range(2):
                        for tt in range(8):
                            t = half * 8 + tt
                            TR(psbf(4 + half)[:, tt * 128:(tt + 1) * 128], kT[:, h, t * 128:(t + 1) * 128], identb[:],
                               [kT, identb], [PS[4 + half]], tt == 7)
                        CP("act", Ktok[:, half * 8:(half + 1) * 8, h * 128:(h + 1) * 128],
                           psbf(4 + half).rearrange("p (t c) -> p t c", t=8), [PS[4 + half]], [Ktok])

            b_in(0)
            b_in(1)
            b_conv(0)
            b_silu(0)
            for c in range(12):
                if c + 2 < 12:
                    b_in(c + 2)
                if c + 1 < 12:
                    b_conv(c + 1)
                b_fin(c)
                if c + 1 < 12:
                    b_silu(c + 1)
            if seq == 0:
                tap("qT", qT); tap("kT", kT); tap("Vtok", Vtok); tap("Ktok", Ktok)
            if stop_after == "B":
                break

            CT0 = O_X + 32768
            mix_all, _ = AR.view(O_X, [8, T], BF16)
            mix_reg = AR.buf("mixreg", O_X, [8, T], BF16)
            mixG = [Buf(mix_all[:, 0:4, t * 128:(t + 1) * 128], "mixG%d" % t, inherits=[mix_reg]) for t in range(NT)]
            mixM = [Buf(mix_all[:, 4:8, t * 128:(t + 1) * 128], "mixM%d" % t, inherits=[mix_reg]) for t in range(NT)]
            for b_ in mixG + mixM:
                AR.track(O_X, 8 * T * 2, b_)
            GM = AR.buf("GM", CT0, [4, 128], F32)
            DMi = AR.buf("DMi", CT0 + 4096, [4, 128], F32)
            nbM = AR.buf("nbM", CT0 + 6144, [4, 128], F32)
            tq = AR.buf("tq", CT0 + 8192, [4, 128], F32)
            Dsc2 = [AR.buf("Dsc", CT0 + 2048, [4, 128], F32), AR.buf("DscB", CT0 + 35840, [4, 128], F32)]
            EGr2 = [AR.buf("EGr", CT0 + 10240, [4, 128], F32), AR.buf("EGrB", CT0 + 37888, [4, 128], F32)]
            Qb = AR.bufs("Qb", CT0 + 12288, 2, [4, 128], BF16)
            Pb = AR.bufs("Pb", CT0 + 14336, 2, [4, 128], BF16)
            Rb = AR.bufs("Rb", CT0 + 16384, 2, [4, 128], BF16)
            ke = AR.buf("ke", CT0 + 18432, [4, 128], BF16)
            ATb2 = AR.bufs("AT", CT0 + 19456, 2, [4, 128], BF16)
            RF2 = AR.bufs("RF", CT0 + 21504, 2, [4, 128], BF16)
            kdec2 = AR.bufs("kdec", CT0 + 23552, 2, [4, 128], BF16)
            qeT2 = AR.bufs("qeT", CT0 + 25600, 2, [4, 128], BF16)
            qeB2 = AR.bufs("qeB", CT0 + 41216, 2, [4, 128], BF16)
            for b_ in qeT2 + qeB2:
                P.op("pool", lambda e, b_=b_: e.memset(b_[:], 0.0), (), [b_])
            nW2 = AR.bufs("nW", CT0 + 27648, 2, [4, 128], BF16)
            vnew = AR.buf("vnew", CT0 + 29696, [4, 128], BF16)
            S32 = AR.buf("S32", CT0 + 30720, [4, 128], F32)
            Sdec = AR.buf("Sdec", CT0 + 32768, [4, 128], F32)
            Sbf = AR.buf("Sbf", CT0 + 34816, [4, 128], BF16)
            sqo = tq
            og1 = DMi
            OGb = AR.buf("OG", CT0 + 39936, [4, 128], BF16)
            ssO = AR.buf("ssO", CT0 + 40960, [4], F32)

            def v4(bank, bf=False):
                a = psbf(bank)[:, 0:512] if bf else PS[bank].ap
                return a.rearrange("p (h c) -> p h c", h=4)

            P.op("dve", lambda e: e.memset(S32[:], 0.0), (), [S32])
            P.op("dve", lambda e: e.memset(Sbf[:], 0.0), (), [Sbf])

            def c_decay_units(t):
                Dsc, EGr = Dsc2[t % 2], EGr2[t % 2]

                def d0():
                    TT(GM[:], bc_mid(maskI, 4), bc_last(gstep[:, t, :], 128), ALU.mult, [cst, gstep], [GM], eng="pool")

                def d1():
                    MM(PS[0][:, :], ones_f, GM[:].rearrange("p h c -> p (h c)"), True, True, [cst, GM], [PS[0]], True)

                def d2():
                    TT(Dsc[:], v4(0), bc_last(gcum[:, t, :], 128), ALU.min, [PS[0], gcum], [Dsc])
                    TT(Dsc[:], Dsc[:], bc_last(gcum[:, t, :], 128), ALU.subtract, [Dsc, gcum], [Dsc])
                    ACT(EGr[:], v4(0), AF.Exp, [PS[0]], [EGr])

                def d3():
                    ACT(Dsc[:], Dsc[:], AF.Exp, [Dsc], [Dsc])
                return [d0, d1, d2, d3]

            def c_prep_units(t):
                tc = slice(t * 128, (t + 1) * 128)
                ATb, RF, kdec, qeT, nW = ATb2[t % 2], RF2[t % 2], kdec2[t % 2], qeT2[t % 2], nW2[t % 2]
                qeB = qeB2[t % 2]
                Dsc, EGr = Dsc2[t % 2], EGr2[t % 2]
                U = []

                def u0():
                    for h in range(4):
                        MM(PS[1][:, h * 128:(h + 1) * 128], kT[:, h, tc], kT[:, h, tc], True, True, [kT], [PS[1]], h == 3)
                    for h in range(4):
                        MM(PS[2][:, h * 128:(h + 1) * 128], kT[:, h, tc], qT[:, h, tc], True, True, [kT, qT], [PS[2]], h == 3)
                    TT(nbM[:], bc_mid(maskS, 4), bc_last(beta[:, t, :], 128), ALU.mult, [cst, beta], [nbM], eng="pool")
                    TT(ke[:], Ktok[:, t, :].rearrange("p (h c) -> p h c", h=4), bc_last(eg[:, t, :], 128), ALU.mult, [Ktok, eg], [ke], eng="pool")
                    TT(kdec[:], Ktok[:, t, :].rearrange("p (h c) -> p h c", h=4), bc_last(egl[:, t, :], 128), ALU.mult, [Ktok, egl], [kdec], eng="pool")
                U.append(u0)

                def u3():
                    TT(qeT[:, :, 0:64], qT[:, :, t * 128:t * 128 + 64], EGr[:, :, 0:64], ALU.mult, [qT, EGr], [qeT])
                    TT(qeB[:, :, 64:128], qT[:, :, t * 128 + 64:t * 128 + 128], EGr[:, :, 64:128], ALU.mult, [qT, EGr], [qeB])
                U.append(u3)

                def u4():
                    TT(tq[:], v4(1), Dsc[:], ALU.mult, [PS[1], Dsc], [tq])
                    STT(Qb[0][:], tq[:], -1.0, nbM[:], ALU.mult, ALU.mult, [tq, nbM], [Qb[0]])
                    TT(DMi[:], Dsc[:], bc_mid(maskI, 4), ALU.mult, [Dsc, cst], [DMi], eng="pool")
                U.append(u4)

                def u5():
                    for h in range(4):
                        TR(psbf(3)[:, h * 128:(h + 1) * 128], Qb[0][:, h, :], identb[:], [Qb[0], identb], [PS[3]], h == 3)
                    TT(Rb[0][:], Qb[0][:], bc_mid(ident_f, 4), ALU.add, [Qb[0], cst], [Rb[0]])
                    TT(ATb[:], v4(2), DMi[:], ALU.mult, [PS[2], DMi], [ATb])
                U.append(u5)

                def u6():
                    CP("act", Pb[0][:], v4(3, True), [PS[3]], [Pb[0]])
                U.append(u6)

                def mk_sq(lv):
                    def f():
                        cur, nxt = lv % 2, 1 - lv % 2
                        for h in range(4):
                            MM(PS[1][:, h * 128:(h + 1) * 128], Qb[cur][:, h, :], Pb[cur][:, h, :], True, True, [Qb[cur], Pb[cur]], [PS[1]], h == 3)
                        if lv < 4:
                            for h in range(4):
                                MM(PS[2][:, h * 128:(h + 1) * 128], Pb[cur][:, h, :], Qb[cur][:, h, :], True, True, [Qb[cur], Pb[cur]], [PS[2]], h == 3)
                        if lv > 0:
                            pc, pn = (lv - 1) % 2, 1 - (lv - 1) % 2
                            for h in range(4):
                                MM(PS[3][:, h * 128:(h + 1) * 128], Pb[pn][:, h, :], Rb[pc][:, h, :], True, False, [Pb[pn], Rb[pc]], [PS[3]], False)
                                MM(PS[3][:, h * 128:(h + 1) * 128], identb[:], Rb[pc][:, h, :], False, True, [identb, Rb[pc]], [PS[3]], h == 3)
                    return f

                def mk_ev(lv):
                    def f():
                        cur, nxt = lv % 2, 1 - lv % 2
                        CP("act", Pb[nxt][:], v4(1), [PS[1]], [Pb[nxt]])
                        if lv < 4:
                            CP("dve", Qb[nxt][:], v4(2), [PS[2]], [Qb[nxt]])
                        if lv > 0:
                            pn = 1 - (lv - 1) % 2
                            CP("dve" if lv == 4 else "act", Rb[pn][:], v4(3), [PS[3]], [Rb[pn]])
                    return f

                for lv in range(5):
                    U.append(mk_sq(lv))
                    U.append(mk_ev(lv))

                def u_rl():
                    for h in range(4):
                        MM(PS[3][:, h * 128:(h + 1) * 128], Pb[1][:, h, :], Rb[0][:, h, :], True, False, [Pb[1], Rb[0]], [PS[3]], False)
                        MM(PS[3][:, h * 128:(h + 1) * 128], identb[:], Rb[0][:, h, :], False, True, [identb, Rb[0]], [PS[3]], h == 3)
                U.append(u_rl)

                def u_rf():
                    CP("dve", RF[:], v4(3), [PS[3]], [RF])
                U.append(u_rf)

                def u_w():
                    for h in range(4):
                        MM(PS[0][:, h * 128:(h + 1) * 128], ke[:, h, :], RF[:, h, :], True, True, [ke, RF], [PS[0]], h == 3)
                U.append(u_w)

                def u_we():
                    ACT(nW[:], v4(0), AF.Copy, [PS[0]], [nW], scale=-1.0)
                U.append(u_we)
                return U

            def c_seq_units(t):
                ATb, RF, kdec, qeT, nW = ATb2[t % 2], RF2[t % 2], kdec2[t % 2], qeT2[t % 2], nW2[t % 2]
                qeB = qeB2[t % 2]
                U = []
                for ci in range(2):
                    rr = slice(ci * 64, ci * 64 + 64)

                    def a(rr=rr):
                        for h in range(4):
                            hs = slice(h * 128, (h + 1) * 128)
                            MM(PS[4][:, hs], RF[rr, h, :], Vtok[rr, t, hs], True, False, [RF, Vtok], [PS[4]], False)
                            MM(PS[4][:, hs], nW[:, h, :], Sbf[:, h, :], False, True, [nW, Sbf], [PS[4]], h == 3)

                    def b(rr=rr, ci=ci):
                        TT(vnew[rr, :, :], PS[4][rr, :].rearrange("p (h c) -> p h c", h=4), bc_last(beta[rr, t, :], 128), ALU.mult,
                           [PS[4], beta], [vnew])
                        TT(Sdec[:], S32[:], bc_last(EGL[:, ci, t, :], 128), ALU.mult, [S32, EGL], [Sdec])

                    def c(rr=rr, ci=ci):
                        for h in range(4):
                            hs = slice(h * 128, (h + 1) * 128)
                            MM(PS[5][:, hs], kdec[rr, h, :], vnew[rr, h, :], True, True, [kdec, vnew], [PS[5]], h == 3)
                        qe = qeT if ci == 0 else qeB
                        for h in range(4):
                            hs = slice(h * 128, (h + 1) * 128)
                            MM(PS[6][:, hs], qe[:, h, :], Sbf[:, h, :], ci == 0 and h == 0, False, [qe, Sbf], [PS[6]], False, sgc=True)
                            MM(PS[6][:, hs], ATb[rr, h, :], vnew[rr, h, :], False, ci == 1, [ATb, vnew], [PS[6]], h == 3, sgc=True)

                    def d():
                        TT(Sbf[:], Sdec[:], v4(5), ALU.add, [Sdec, PS[5]], [Sbf])
                        TT(S32[:], Sdec[:], v4(5), ALU.add, [Sdec, PS[5]], [S32])
                    U += [a, b, c, d]

                def e():
                    ACT(sqo[:], v4(6), AF.Square, [PS[6]], [sqo])
                U.append(e)

                def f():
                    P.op("dve", lambda e_: e_.tensor_reduce(out=ssO[:], in_=sqo[:], axis=AX.X, op=ALU.add), [sqo], [ssO])
                U.append(f)

                def g_():
                    ACT(ssO[:], ssO[:], AF.Ln, [ssO, eps_t], [ssO], scale=1.0 / 128, bias=eps_t[:])
                    ACT(ssO[:], ssO[:], AF.Exp, [ssO], [ssO], scale=-0.5)
                U.append(g_)

                def h_():
                    TT(og1[:], v4(6), bc_last(ssO[:], 128), ALU.mult, [PS[6], ssO], [og1])
                    TT(OGb[:], og1[:], Zs[:, t, :].rearrange("p (h c) -> p h c", h=4), ALU.mult, [og1, Zs], [OGb], eng="pool")
                U.append(h_)

                def i_():
                    for h in range(4):
                        TR(psbf(7)[:, h * 128:(h + 1) * 128], OGb[:, h, :], identb[:], [OGb, identb], [PS[7]], h == 3)
                U.append(i_)

                def j_():
                    CP("act", mixG[t][:], v4(7, True), [PS[7]], [mixG[t]])
                U.append(j_)
                return U

            for f_ in c_decay_units(0):
                f_()
            for f_ in c_prep_units(0):
                f_()
            for f_ in c_decay_units(1):
                f_()
            DEC0 = 12
            for t in range(NT):
                pu = c_prep_units(t + 1) if t + 1 < NT else []
                su = c_seq_units(t)
                du = c_decay_units(t + 2) if t + 2 < NT else []
                n_ = max(len(pu), len(su), DEC0 + len(du))
                for i_u in range(n_):
                    if i_u < len(su):
                        su[i_u]()
                    if i_u < len(pu):
                        pu[i_u]()
                    if DEC0 <= i_u < DEC0 + len(du):
                        du[i_u - DEC0]()
            if stop_after == "C":
                if seq == 0:
                    tap("mixT", mix_reg, mix_all, deps=mixG)
                break

            mqT = AR.buf("mqT", O_G, [4, T], BF16)
            mkT = AR.buf("mkT", O_G + 16384, [4, T], BF16)
            Vp = AR.buf("Vp", O_G + 32768, [NT, 4, 130], BF16)
            sel = AR.buf("sel", O_G + 49408, [NT, 4, 8], F32)
            kmean = AR.buf("kmean", O_G + 51456, [4, 8], F32)
            ksum = AR.buf("ksum", O_G + 51584, [NT, 4], F32)
            ksum2 = AR.buf("ksum2", O_G + 51840, [8, 4], F32)
            wgm = AR.bufs("wgm", CT0, 2, [8, 512], BF16)
            stg = AR.bufs("stg", CT0 + 16384, 2, [4, 128], F32) + [AR.buf("stg2", CT0 + 28672, [4, 128], F32)]
            qT32 = AR.buf("qT32", CT0 + 20480, [4, 128], F32)
            qT32l = [qT32, AR.buf("qT32b", CT0 + 34816, [4, 128], F32)]
            gate_sb = AR.buf("gate_sb", CT0 + 22528, [4, 8], F32)
            top8 = AR.buf("top8", CT0 + 22656, [4, 8], F32)
            rt = AR.bufs("rt", CT0 + 22784, 4, [4, 16], F32)
            PT = AR.bufs("PT", CT0 + 30720, 6, [256], BF16)
            PTall, _ = AR.view(CT0 + 30720, [3, 512], BF16)
            Oacc = AR.buf("Oacc", CT0 + 25856, [2, 132], F32)
            rden = AR.buf("rden", CT0 + 26912, [2], F32)
            omb = AR.buf("omb", CT0 + 26920, [2, 128], BF16)

            def rotary(sb, t):
                x1 = sb[:, :, 0:16]
                x2 = sb[:, :, 16:32]
                cs = bc_mid(cos_t[:, t, :], 4)
                sn = bc_mid(sin_t[:, t, :], 4)
                TT(rt[0][:], x1, cs, ALU.mult, [sb, cst], [rt[0]])
                TT(rt[1][:], x2, sn, ALU.mult, [sb, cst], [rt[1]])
                TT(rt[2][:], x2, cs, ALU.mult, [sb, cst], [rt[2]])
                TT(rt[3][:], x1, sn, ALU.mult, [sb, cst], [rt[3]])
                TT(x1, rt[0][:], rt[1][:], ALU.subtract, [rt[0], rt[1]], [sb])
                TT(x2, rt[2][:], rt[3][:], ALU.add, [rt[2], rt[3]], [sb])

            P.dma("sp", wgm[0][:], wview(wb_in, 2568, 512), reads=WB_IN, writes=[wgm[0]])
            P.dma("sp", wgm[1][:], wview(wb_in, 3080, 512), reads=WB_IN, writes=[wgm[1]])
            P.op("dve", lambda e: e.memset(Vp[:, :, :, 128:130], 1.0), (), [Vp])

            def d_mm(t, w, sb3):
                b = t % 2
                for k in range(8):
                    MM(PS[b][:, :], uT[t][:, k, :], w[:, k, :], k == 0, k == 7, [uT[t], w], [PS[b]], k == 7)
                CP("act", sb3[:], v4(b), [PS[b]], [sb3])
                rotary(sb3, t)

            def d_ktr(t, sb3):
                b = t % 2
                for h in range(4):
                    TR(PS[2 + b][:, h * 128:(h + 1) * 128], sb3[:, h, :], ident_f, [sb3, cst], [PS[2 + b]], h == 3)
                CP("act", mkT[:, :, t * 128:(t + 1) * 128], v4(2 + b), [PS[2 + b]], [mkT])
                P.op("dve", lambda e: e.tensor_reduce(out=ksum[:, t, :], in_=v4(2 + b), axis=AX.X, op=ALU.add),
                     [PS[2 + b]], [ksum])

            def d_qtr(t, sb3):
                b = t % 2
                qb = t // 2
                for h in range(4):
                    TR(PS[2 + b][:, h * 128:(h + 1) * 128], sb3[:, h, :], ident_f, [sb3, cst], [PS[2 + b]], h == 3)
                ACT(mqT[:, :, t * 128:(t + 1) * 128], v4(2 + b), AF.Copy, [PS[2 + b]], [mqT], scale=float(128.0 ** -0.5))
                if qb >= 4:
                    CP("dve", qT32l[t % 2][:], v4(2 + b), [PS[2 + b]], [qT32l[t % 2]])

            def d_gate(t):
                qb = t // 2
                if qb < 4:
                    return
                q32 = qT32l[t % 2]
                for h in range(4):
                    MM(PS[4][:, h * 8:(h + 1) * 8], q32[:, h, :], kmean[:, h, :], True, True, [q32, kmean], [PS[4]], h == 3)
                CP("dve", gate_sb[:], PS[4][:, 0:32].rearrange("p (h n) -> p h n", h=4), [PS[4]], [gate_sb])
                P.op("dve", lambda e: e.memset(gate_sb[:, :, qb:8], -1e30), (), [gate_sb])
                for h in range(4):
                    P.op("dve", lambda e, h=h: e.max(out=top8[:, h, :], in_=gate_sb[:, h, :]), [gate_sb], [top8])
                TT(sel[:, t, :, :], gate_sb[:], bc_last(top8[:, :, 2], 8), ALU.is_ge, [gate_sb, top8], [sel])

            d_mm(0, wgm[0], stg[0])
            for t in range(NT):
                if t + 1 < NT:
                    d_mm(t + 1, wgm[0], stg[(t + 1) % 3])
                d_ktr(t, stg[t % 3])
            ks4 = ksum[:].rearrange("p (n two) h -> p n two h", two=2)
            TT(ksum2[:], ks4[:, :, 0, :], ks4[:, :, 1, :], ALU.add, [ksum], [ksum2])
            TS(kmean[:].rearrange("p h n -> p n h"), ksum2[:], 1.0 / 256, None, ALU.mult, None, [ksum2], [kmean])
            if stop_after == "D1":
                break
            for t in range(NT):
                b = t % 2
                for k in range(8):
                    MM(PS[b][:, :], uT[t][:, k, :], wgm[1][:, k, :], k == 0, k == 7, [uT[t], wgm[1]], [PS[b]], k == 7)
                CP("act", Vp[:, t, :, 0:128], v4(b), [PS[b]], [Vp])
            P.dma("sp", wgm[0][:], wview(wb_in, 2056, 512), reads=WB_IN, writes=[wgm[0]])
            d_mm(0, wgm[0], stg[0])
            for t in range(NT):
                if t + 1 < NT:
                    d_mm(t + 1, wgm[0], stg[(t + 1) % 3])
                d_qtr(t, stg[t % 3])
                if t >= 1:
                    d_gate(t - 1)
            d_gate(NT - 1)
            if seq == 0:
                tap("mqT", mqT); tap("mkT", mkT); tap("sel", sel); tap("Vp", Vp)
            if stop_after == "D2":
                break
            steps = []
            for h in range(4):
                for qb in range(8):
                    blocks = [qb] + list(range(qb))
                    for bi, n in enumerate(blocks):
                        steps.append((h, qb, n, bi == 0, bi == len(blocks) - 1))

            def att_st(i):
                h, qb, n, own, last = steps[i]
                r = i % 3
                bank = PS[r]
                sA = bank[:, 0:256]
                sB = bank[:, 256:512]
                pA, pB = PT[2 * r], PT[2 * r + 1]
                qc = slice(qb * 256, (qb + 1) * 256)
                k0 = n * 256
                if own:
                    MM(sA, mkT[:, h, k0:k0 + 128], mqT[:, h, qc], True, True, [mkT, mqT], [bank], False)
                    MM(sB[:, 0:128], mkT[:, h, k0 + 128:k0 + 256], mqT[:, h, qb * 256 + 128:qb * 256 + 256], True, True,
                       [mkT, mqT], [bank], True)
                    ACT(pA[:], sA, AF.Exp, [bank], [pA])
                    ACT(pB[:, 0:128], sB[:, 0:128], AF.Exp, [bank], [pB])
                    TT(pA[:, 0:128], pA[:, 0:128], triU, ALU.mult, [pA, cst], [pA], eng="pool")
                    TT(pB[:, 0:128], pB[:, 0:128], triU, ALU.mult, [pB, cst], [pB], eng="pool")
                else:
                    MM(sA, mkT[:, h, k0:k0 + 128], mqT[:, h, qc], True, True, [mkT, mqT], [bank], False)
                    MM(sB, mkT[:, h, k0 + 128:k0 + 256], mqT[:, h, qc], True, True, [mkT, mqT], [bank], True)
                    ACT(PTall[:, r, :], bank[:, :], AF.Exp, [bank], [pA, pB])

            def att_pv(i):
                h, qb, n, own, last = steps[i]
                r = i % 3
                pA, pB = PT[2 * r], PT[2 * r + 1]
                ob = 3 + r
                OV = PS[ob][:, 0:264].rearrange("p (q c) -> p q c", q=2)
                if own:
                    MM(OV[:, 0, 0:129], pA[:, 0:128], Vp[:, 2 * n, h, 0:129], True, True, [pA, Vp], [PS[ob]], False)
                    MM(OV[:, 1, 0:129], pA[:, 128:256], Vp[:, 2 * n, h, 0:129], True, False, [pA, Vp], [PS[ob]], False)
                    MM(OV[:, 1, 0:129], pB[:, 0:128], Vp[:, 2 * n + 1, h, 0:129], False, True, [pB, Vp], [PS[ob]], True)
                    CP("dve", Oacc[:, :, 0:129], OV[:, :, 0:129], [PS[ob]], [Oacc])
                else:
                    for qt in range(2):
                        MM(OV[:, qt, 0:129], pA[:, qt * 128:(qt + 1) * 128], Vp[:, 2 * n, h, 0:129], True, False, [pA, Vp], [PS[ob]], False)
                        MM(OV[:, qt, 0:129], pB[:, qt * 128:(qt + 1) * 128], Vp[:, 2 * n + 1, h, 0:129], False, True, [pB, Vp], [PS[ob]], qt == 1)
                    if qb >= 4:
                        for qt in range(2):
                            STT(Oacc[:, qt, 0:129], OV[:, qt, 0:129], sel[:, 2 * qb + qt, h, n:n + 1], Oacc[:, qt, 0:129],
                                ALU.mult, ALU.add, [PS[ob], sel, Oacc], [Oacc])
                    else:
                        TT(Oacc[:, :, 0:129], Oacc[:, :, 0:129], OV[:, :, 0:129], ALU.add, [Oacc, PS[ob]], [Oacc])
                if last:
                    ob_ = ombs[qb % 2]
                    P.op("dve", lambda e: e.reciprocal(rden[:], Oacc[:, :, 128]), [Oacc], [rden])
                    TT(ob_[:], Oacc[:, :, 0:128], bc_last(rden[:], 128), ALU.mult, [Oacc, rden], [ob_])
                    pending.append((i + 2, h, qb, ob_))

            def att_tail(i, force=False):
                while pending and (force or pending[0][0] <= i):
                    _, h, qb, ob_ = pending.pop(0)
                    tb = 6 + qb % 2
                    for qt in range(2):
                        TR(psbf(tb)[:, qt * 128:(qt + 1) * 128], ob_[:, qt, :], identb[:], [ob_, identb], [PS[tb]], qt == 1)
                    for qt in range(2):
                        CP("act", mixM[2 * qb + qt][:, h, :], psbf(tb)[:, qt * 128:(qt + 1) * 128], [PS[tb]], [mixM[2 * qb + qt]])

            ns = len(steps)
            pending = []
            ombs = [omb, AR.buf("ombB", CT0 + 33792, [2, 128], BF16)]
            att_st(0)
            att_st(1)
            for i in range(ns):
                if i + 2 < ns:
                    att_st(i + 2)
                att_pv(i)
                att_tail(i)
            att_tail(ns, force=True)
            if stop_after == "D":
                if seq == 0:
                    tap("mixT", mix_reg, mix_all, deps=mixG + mixM)
                break

            EB = O_U
            h1 = AR.bufs("h1", EB, 4, [D], F32)
            fsb = AR.bufs("fsb", EB + 16384, 4, [D], F32)
            nT_all, _ = AR.view(EB + 32768, [8, 512], BF16)
            nT_reg = AR.buf("nTreg", EB + 32768, [8, 512], BF16)
            hid_all, _ = AR.view(EB + 40960, [32, 512], BF16)
            hid_reg = AR.buf("hidreg", EB + 40960, [32, 512], BF16)
            wbuf = AR.bufs("wbuf", EB + 73728, 3, [8, 512], BF16)
            npost_b = AR.buf("npost_b", EB + 98304, [D], F32)
            mpost_b = AR.buf("mpost_b", EB + 102400, [D], F32)
            pT_all, _ = AR.view(EB + 106496, [2, 512], BF16)
            pT_reg = AR.buf("pTreg", EB + 106496, [2, 512], BF16)
            ssE = AR.bufs("ssE", EB + 108544, 4, [1], F32)
            pb16 = AR.buf("pb16", EB + 108576, [256], BF16)
            wdn = AR.bufs("wdn", CT0, 2, [32, 256], BF16)
            if seq == 0:
                for hf in range(2):
                    ftmp = AR.buf("ftmp%d" % hf, CT0 + 32768 + hf * 4096, [4, 512], BF16)
                    P.dma("pool", ftmp[:], wb_out[hf][:, 0:4, :], reads=[WB["out"][hf]], writes=[ftmp])
                    TS(ftmp[:], ftmp[:], gnw_c[:, 0:1], None, ALU.mult, None, [ftmp, gnw_c], [ftmp])
                    P.dma("pool", wb_out[hf][:, 0:4, :], ftmp[:], reads=[ftmp], writes=[WB["out"][hf]])
            xt2l = [AR.buf("xt2", CT0 + 32768, [D], F32), AR.buf("xt2b", CT0 + 36864, [D], F32)]
            junkE = AR.buf("junkE", EB + 113408, [D], BF16)
            ssF = AR.bufs("ssF", EB + 115456, 4, [1], F32)
            pb16l = [pb16, AR.buf("pb16b", EB + 115472, [256], BF16)]
            sqE = [AR.buf("sqE0", CT0 + 40960, [512], F32), AR.buf("sqE1", EB + 109312, [512], F32)]
            hb3 = [AR.buf("hb", CT0 + 43008, [D], BF16), AR.buf("hbB", EB + 111360, [D], BF16), AR.buf("hbC", EB + 115984, [D], BF16)]
            P.dma("sp", npost_b[:], n_post_d.partition_broadcast(128), writes=[npost_b])
            P.dma("sp", mpost_b[:], m_post_d.partition_broadcast(128), writes=[mpost_b])
            wrot = [0]

            def next_w():
                w = wbuf[wrot[0] % 3]
                wrot[0] += 1
                return w

            def rms_scale(src_ap, srcbuf, ssb, junkbuf):
                ACT(junkbuf[:], src_ap, AF.Square, [srcbuf], [junkbuf, ssb], accum_out=ssb[:])
                ACT(ssb[:], ssb[:], AF.Sqrt, [ssb, eps_t], [ssb], scale=1.0 / D, bias=eps_t[:])
                P.op("dve", lambda e: e.reciprocal(ssb[:], ssb[:]), [ssb], [ssb])

            nT = [Buf(nT_all[:, :, i * 128:(i + 1) * 128], "nT%d" % i, inherits=[nT_reg]) for i in range(4)]
            hid = [Buf(hid_all[:, j, :], "hid%d" % j, inherits=[hid_reg]) for j in range(32)]
            pT = [Buf(pT_all[:, :, i * 128:(i + 1) * 128], "pT%d" % i, inherits=[pT_reg]) for i in range(4)]
            for b_ in nT:
                AR.track(EB + 32768, 8192, b_)
            for b_ in hid:
                AR.track(EB + 40960, 32768, b_)
            for b_ in pT:
                AR.track(EB + 106496, 2048, b_)
            hidS = [[Buf(hid_all[:, j, ub * 256:(ub + 1) * 256], "hid%d_%d" % (ub, j), inherits=[hid[j]]) for j in range(32)]
                    for ub in range(2)]
            for ub in range(2):
                for b_ in hidS[ub]:
                    AR.track(EB + 40960, 32768, b_)

            sqH = [Buf(sqE[0][:, 0:256], "sqH0", inherits=[sqE[0]]), Buf(sqE[0][:, 256:512], "sqH1", inherits=[sqE[0]]),
                   Buf(sqE[1][:, 0:256], "sqH2", inherits=[sqE[1]]), Buf(sqE[1][:, 256:512], "sqH3", inherits=[sqE[1]])]
            AR.track(CT0 + 40960, 2048, sqH[0]); AR.track(CT0 + 40960, 2048, sqH[1])
            AR.track(EB + 109312, 2048, sqH[2]); AR.track(EB + 109312, 2048, sqH[3])

            def zip_units(lists):
                n_ = max(len(l) for l in lists)
                for ui in range(n_):
                    for l in lists:
                        if ui < len(l):
                            l[ui]()

            def e1b_units(t, i):
                hbb = hb3[i % 3]
                xt2 = xt2l[i % 2]
                tb = 2 if i % 2 == 0 else 7
                return [
                    lambda: (P.dma("pool", xt2[:], x_d[seq, t * 128:(t + 1) * 128, :], writes=[xt2]),
                             ACT(junkE[:], fsb[i][:], AF.Square, [fsb[i]], [junkE, ssE[i]], accum_out=ssE[i][:])),
                    lambda: ACT(ssE[i][:], ssE[i][:], AF.Sqrt, [ssE[i], eps_t], [ssE[i]], scale=1.0 / D, bias=eps_t[:]),
                    lambda: P.op("dve", lambda e: e.reciprocal(ssE[i][:], ssE[i][:]), [ssE[i]], [ssE[i]]),
                    lambda: STT(fsb[i][:], fsb[i][:], ssE[i][:], npost_b[:], ALU.mult, ALU.mult, [fsb[i], ssE[i], npost_b], [fsb[i]]),
                    lambda: TT(h1[i][:], fsb[i][:], xt2[:], ALU.add, [fsb[i], xt2], [h1[i]]),
                    lambda: ACT(junkE[:], h1[i][:], AF.Square, [h1[i]], [junkE, ssF[i]], accum_out=ssF[i][:]),
                    lambda: ACT(ssF[i][:], ssF[i][:], AF.Sqrt, [ssF[i], eps_t], [ssF[i]], scale=1.0 / D, bias=eps_t[:]),
                    lambda: P.op("dve", lambda e: e.reciprocal(ssF[i][:], ssF[i][:]), [ssF[i]], [ssF[i]]),
                    lambda: ACT(hbb[:], h1[i][:], AF.Copy, [h1[i], ssF[i]], [hbb], scale=ssF[i][:]),
                    lambda: [TR(psbf(tb)[:, k * 128:(k + 1) * 128], hbb[:, k * 128:(k + 1) * 128], identb[:], [hbb, identb], [PS[tb]], k == 7)
                             for k in range(8)],
                    lambda: TT(nT[i][:], psbf(tb).rearrange("p (k c) -> p k c", k=8), bc_last(wmlp_c[:], 128), ALU.mult,
                               [PS[tb], wmlp_c], [nT[i]]),
                ]

            def e4_units(t, i):
                hbb = hb3[i % 3]
                xt2 = xt2l[i % 2]
                pbb = pb16l[i % 2]
                tb = 2 if i % 2 == 0 else 7
                return [
                    lambda: (P.dma("pool", xt2[:, 0:256], p_d[seq, t * 128:(t + 1) * 128, :], writes=[xt2]),
                             ACT(junkE[:], fsb[i][:], AF.Square, [fsb[i]], [junkE, ssE[i]], accum_out=ssE[i][:])),
                    lambda: ACT(ssE[i][:], ssE[i][:], AF.Sqrt, [ssE[i], eps_t], [ssE[i]], scale=1.0 / D, bias=eps_t[:]),
                    lambda: P.op("dve", lambda e: e.reciprocal(ssE[i][:], ssE[i][:]), [ssE[i]], [ssE[i]]),
                    lambda: STT(fsb[i][:], fsb[i][:], ssE[i][:], mpost_b[:], ALU.mult, ALU.mult, [fsb[i], ssE[i], mpost_b], [fsb[i]]),
                    lambda: (TT(h1[i][:], h1[i][:], fsb[i][:], ALU.add, [h1[i], fsb[i]], [h1[i]]),
                             CP("act", pbb[:], xt2[:, 0:256], [xt2], [pbb])),
                    lambda: CP("act", hbb[:], h1[i][:], [h1[i]], [hbb]),
                    lambda: [TR(psbf(tb)[:, k * 128:(k + 1) * 128], hbb[:, k * 128:(k + 1) * 128], identb[:], [hbb, identb], [PS[tb]], k == 7)
                             for k in range(8)],
                    lambda: CP("dve", nT[i][:], psbf(tb).rearrange("p (k c) -> p k c", k=8), [PS[tb]], [nT[i]]),
                    lambda: [TR(psbf(tb)[:, k * 128:(k + 1) * 128], pbb[:, k * 128:(k + 1) * 128], identb[:], [pbb, identb], [PS[tb]], k == 1)
                             for k in range(2)],
                    lambda: CP("dve", pT[i][:], psbf(tb)[:, 0:256].rearrange("p (k c) -> p k c", k=2), [PS[tb]], [pT[i]]),
                ]

            def st_E1(u):
                ub = u % 2
                for half in range(2):
                    wo = next_w()
                    P.dma("sp", wo[:], wb_out[half], reads=[WB["out"][half]], writes=[wo])
                    for il in range(2):
                        t, i = u * 2 + il, ub * 2 + il
                        for j in range(8):
                            MM(PS[il][:, :], mix_all[:, j, t * 128:(t + 1) * 128], wo[:, j, :], j == 0, j == 7,
                               [mixG[t], mixM[t], wo], [PS[il]], j == 7)
                        CP("act", fsb[i][:, half * 512:(half + 1) * 512], PS[il][:, :], [PS[il]], [fsb[i]])

            def zipped(lists):
                out = []
                n_ = max(len(l) for l in lists)
                for ui in range(n_):
                    for l in lists:
                        if ui < len(l):
                            out.append(l[ui])
                return out

            def e1b_list(u):
                ub = u % 2
                return zipped([e1b_units(u * 2, ub * 2), e1b_units(u * 2 + 1, ub * 2 + 1)])

            def e4_list(u):
                ub = u % 2
                return zipped([e4_units(u * 2, ub * 2), e4_units(u * 2 + 1, ub * 2 + 1)])

            def st_E2(u, extra=()):
                extra = list(extra)
                ub = u % 2
                cs = slice(ub * 256, (ub + 1) * 256)
                wu = next_w()
                P.dma("sp", wu[:], wb_up[0], reads=[WB["up"][0]], writes=[wu])
                for cg in range(8):
                    wcur = wu
                    if cg < 7:
                        wu = next_w()
                        P.dma("sp", wu[:], wb_up[cg + 1], reads=[WB["up"][cg + 1]], writes=[wu])
                    for jj in range(4):
                        j = cg * 4 + jj
                        pb = 3 + j % 4
                        sqb = sqH[j % 4]
                        for k in range(8):
                            MM(PS[pb][:, 0:256], wcur[:, k, jj * 128:(jj + 1) * 128], nT_all[:, k, cs], k == 0, k == 7,
                               [wcur, nT[ub * 2], nT[ub * 2 + 1]], [PS[pb]], k == 7)
                        ACT(sqb[:], PS[pb][:, 0:256], AF.Square, [PS[pb]], [sqb])
                        STT(hidS[ub][j][:], PS[pb][:, 0:256], 0.0, sqb[:], ALU.is_gt, ALU.mult, [PS[pb], sqb], [hidS[ub][j]])
                        if extra:
                            extra.pop(0)()
                for f_ in extra:
                    f_()

            def st_E3(u, extra=()):
                extra = list(extra)
                ub = u % 2
                wd = wdn[0]
                P.dma("sp", wd[:], wb_down[0], reads=[WB["down"][0]], writes=[wd])
                for qq in range(4):
                    wcur = wd
                    if qq < 3:
                        wd = wdn[(qq + 1) % 2]
                        P.dma("sp", wd[:], wb_down[qq + 1], reads=[WB["down"][qq + 1]], writes=[wd])
                    for il in range(2):
                        i = ub * 2 + il
                        pb = 5 + il
                        for j in range(32):
                            MM(PS[pb][:, 0:256], hid_all[:, j, i * 128:(i + 1) * 128], wcur[:, j, :], j == 0, j == 31,
                               [hidS[ub][j], wcur], [PS[pb]], j == 31)
                        CP("act", fsb[i][:, qq * 256:(qq + 1) * 256], PS[pb][:, 0:256], [PS[pb]], [fsb[i]])
                        for _ in range(3):
                            if extra:
                                extra.pop(0)()
                for f_ in extra:
                    f_()

            def st_E45(u):
                ub = u % 2
                wgts = []
                for half in range(2):
                    wgt = next_w()
                    P.dma("sp", wgt[:], wb_gate[half], reads=[WB["gate"][half]], writes=[wgt])
                    wgts.append(wgt)
                wpl = next_w()
                wplv = wpl[:].rearrange("p k c -> p (k c)")[:, 0:2048].rearrange("p (k c) -> p k c", k=2)
                P.dma("sp", wplv, wb_ple, reads=WB["ple"], writes=[wpl])
                for half in range(2):
                    wgt = wgts[half]
                    hs = slice(half * 512, (half + 1) * 512)
                    for il in range(2):
                        t, i = u * 2 + il, ub * 2 + il
                        for k in range(8):
                            MM(PS[il][:, :], nT[i][:, k, :], wgt[:, k, :], k == 0, k == 7, [nT[i], wgt], [PS[il]], k == 7)
                        for k in range(2):
                            MM(PS[5 + il][:, :], pT[i][:, k, :], wplv[:, k, hs], k == 0, k == 1, [pT[i], wpl], [PS[5 + il]], k == 1)
                        ACT(fsb[i][:, hs], PS[il][:, :], AF.Sigmoid, [PS[il]], [fsb[i]])
                        TT(fsb[i][:, hs], fsb[i][:, hs], PS[5 + il][:, :], ALU.mult, [fsb[i], PS[5 + il]], [fsb[i]])
                        if half == 1:
                            TT(h1[i][:], h1[i][:], fsb[i][:], ALU.add, [fsb[i], h1[i]], [h1[i]], eng="pool")
                            ob_ = Buf(out_d[seq, t * 128:(t + 1) * 128, :], "out")
                            P.dma("pool", out_d[seq, t * 128:(t + 1) * 128, :], h1[i][:], reads=[h1[i]], writes=[ob_])
                            outbufs.append(ob_)

            NSUB = T // 256
            st_E1(0)
            for f_ in e1b_list(0):
                f_()
            st_E2(0)
            st_E1(1)
            for u in range(NSUB):
                st_E3(u, e1b_list(u + 1) if u + 1 < NSUB else ())
                if u + 1 < NSUB:
                    st_E2(u + 1, e4_list(u))
                else:
                    for f_ in e4_list(u):
                        f_()
                st_E45(u)
                if u + 2 < NSUB:
                    st_E1(u + 2)

        if stop_after is None:
            P.wait_all("sp", outbufs + list(tap_out.values()))
        else:
            P.wait_all("sp", list(tap_out.values()))
        P.emit()
    nc._mk_stats = (P.nops, P.nwaits)
    return nc


_NC_CACHE = {}


def kernel(x, p, w_in, conv_w, a_log, dt_bias, gdn_norm_w, w_out, attn_pre_norm,
           attn_post_norm, mlp_pre_norm, mlp_post_norm, w_up, w_down, w_ple, w_ple_gate):
    n_cores = 8
    nseq = 2
    if "nc" not in _NC_CACHE:
        _NC_CACHE["nc"] = build(nseq=nseq)
    nc = _NC_CACHE["nc"]
    f = lambda a: np.ascontiguousarray(np.asarray(a, dtype=np.float32))
    shared = {
        "w_in": f(w_in)[0], "conv_w": f(conv_w)[0], "a_log": f(a_log)[0], "dt_bias": f(dt_bias)[0],
        "gdn_norm_w": f(gdn_norm_w)[0], "w_out": f(w_out)[0], "attn_pre_norm": f(attn_pre_norm)[0],
        "attn_post_norm": f(attn_post_norm)[0], "mlp_pre_norm": f(mlp_pre_norm)[0],
        "mlp_post_norm": f(mlp_post_norm)[0], "w_up": f(w_up)[0], "w_down": f(w_down)[0],
        "w_ple": f(w_ple)[0], "w_ple_gate": f(w_ple_gate)[0], "consts": make_consts(),
    }
    x = f(x)
    p = f(p)[0]
    in_maps = []
    for c in range(n_cores):
        m = dict(shared)
        m["x"] = np.ascontiguousarray(x[c * nseq:(c + 1) * nseq])
        m["p"] = np.ascontiguousarray(p[c * nseq:(c + 1) * nseq])
        in_maps.append(m)
    res = run_bass_kernel_spmd(nc, in_maps, core_ids=list(range(n_cores)))
    out = np.concatenate([np.asarray(r["out"], dtype=np.float32) for r in res.results], axis=0)
    return out
```

```python
import numpy as np
from contextlib import ExitStack
import concourse.bass as bass
import concourse.mybir as mybir
from concourse.bass_utils import run_bass_kernel_spmd

F32 = mybir.dt.float32
BF16 = mybir.dt.bfloat16
AF = mybir.ActivationFunctionType
ALU = mybir.AluOpType
AX = mybir.AxisListType

T = 2048
D = 1024
NT = T // 128
INP = 3592
DFF = 4096
EPS = 1e-6


class Buf:
    __slots__ = ("ap", "w", "r", "name", "excl")

    def __init__(self, ap, name="", inherits=(), excl=False):
        self.ap = ap
        self.name = name
        self.excl = excl
        self.w = {}
        self.r = {}
        for b in inherits:
            for k, v in b.w.items():
                self.w[k] = max(self.w.get(k, 0), v)
            for k, v in b.r.items():
                self.r[k] = max(self.r.get(k, 0), v)

    def __getitem__(self, key):
        return self.ap[key]


class Prog:
    ENG = ("pe", "act", "dve", "pool", "sp")

    def __init__(self, nc, stack, n_dma_sems=32):
        self.nc = nc
        self.lists = {e: [] for e in self.ENG}
        self.cnt = {e: 0 for e in self.ENG}
        self.semobj = {}
        for e in self.ENG:
            self.semobj[("eng", e)] = stack.enter_context(nc.semaphore("s_" + e))
        self.dma_sems = {"sp": [], "pool": []}
        for q, n in (("sp", n_dma_sems), ("pool", 16)):
            for i in range(n):
                key = ("dma" + q, i)
                self.semobj[key] = stack.enter_context(nc.semaphore("s_dma_%s%d" % (q, i)))
                self.dma_sems[q].append([key, 0])
        self.dma_rr = {"sp": 0, "pool": 0}
        self.seen = {e: {} for e in self.ENG}
        self.nops = 0
        self.nwaits = 0

    def _deps(self, eng, reads, writes):
        deps = {}
        me = ("eng", eng)
        for b in reads:
            for k, v in b.w.items():
                if deps.get(k, 0) < v:
                    deps[k] = v
            if b.excl:
                for k, v in b.r.items():
                    if k != me and deps.get(k, 0) < v:
                        deps[k] = v
        for b in writes:
            for k, v in b.w.items():
                if deps.get(k, 0) < v:
                    deps[k] = v
            for k, v in b.r.items():
                if deps.get(k, 0) < v:
                    deps[k] = v
        waits = []
        seen = self.seen[eng]
        for k, v in deps.items():
            if eng == "pe" and k == ("eng", "pe"):
                continue
            if seen.get(k, 0) >= v:
                continue
            seen[k] = v
            waits.append((k, v))
        return waits

    @staticmethod
    def _mark(reads, writes, key, val):
        for b in reads:
            if b.r.get(key, 0) < val:
                b.r[key] = val
        for b in writes:
            b.w = {key: val}
            b.r = {}

    def op(self, eng, fn, reads=(), writes=(), inc=True):
        waits = self._deps(eng, reads, writes)
        key = ("eng", eng)
        if inc:
            self.cnt[eng] += 1
            val = self.cnt[eng]
        else:
            val = self.cnt[eng] + 1
        self._mark(reads, writes, key, val)
        self.lists[eng].append((waits, fn, (key, 1) if inc else None))
        self.nops += 1
        self.nwaits += len(waits)

    def dma(self, eng, out, in_, reads=(), writes=(), **kw):
        pool_ = self.dma_sems[eng]
        slot = pool_[self.dma_rr[eng]]
        self.dma_rr[eng] = (self.dma_rr[eng] + 1) % len(pool_)
        key, uses = slot
        waits = self._deps(eng, reads, writes)
        if uses > 0 and self.seen[eng].get(key, 0) < 16 * uses:
            self.seen[eng][key] = 16 * uses
            waits.append((key, 16 * uses))
        slot[1] = uses + 1
        val = 16 * (uses + 1)
        self._mark(reads, writes, key, val)

        def fn(e, out=out, in_=in_, kw=kw):
            return e.dma_start(out=out, in_=in_, **kw)

        self.lists[eng].append((waits, fn, (key, 16)))
        self.nops += 1
        self.nwaits += len(waits)

    def wait_all(self, eng, bufs):
        waits = self._deps(eng, bufs, ())
        self.lists[eng].append((waits, None, None))

    def emit(self):
        nc = self.nc
        engmap = {"pe": "tensor", "act": "scalar", "dve": "vector",
                  "pool": "gpsimd", "sp": "sync"}
        semobj = self.semobj
        with nc.Block() as block:
            for e in self.ENG:
                lst = self.lists[e]

                def body(engine, lst=lst):
                    for waits, fn, inc in lst:
                        for k, v in waits:
                            engine.wait_ge(semobj[k], v)
                        if fn is None:
                            continue
                        ins = fn(engine)
                        if inc is not None:
                            ins.then_inc(semobj[inc[0]], inc[1])

                getattr(block, engmap[e])(body)


class Arena:
    def __init__(self, ap, nbytes):
        self.ap = ap
        self.nbytes = nbytes
        self.live = []

    def view(self, off, shape, dt):
        esz = 2 if dt == BF16 else 4
        n = 1
        for s in shape:
            n *= s
        size = n * esz
        assert off % 4 == 0 and size % 4 == 0, (off, size)
        assert off + size <= self.nbytes, ("arena overflow", off, size)
        v = self.ap[:, off // 4:(off + size) // 4]
        if dt != F32:
            v = v.bitcast(dt)
        if len(shape) == 2:
            v = v.rearrange("p (a b) -> p a b", a=shape[0])
        elif len(shape) == 3:
            v = v.rearrange("p (a b c) -> p a b c", a=shape[0], b=shape[1])
        return v, size

    def buf(self, name, off, shape, dt):
        v, size = self.view(off, shape, dt)
        end = off + size
        inh = [b for (o, e, b) in self.live if o < end and off < e]
        nb = Buf(v, name, inherits=inh)
        self.live = [(o, e, b) for (o, e, b) in self.live if not (off <= o and e <= end)]
        self.live.append((off, end, nb))
        return nb

    def track(self, off, size, b):
        self.live.append((off, off + size, b))

    def bufs(self, name, off, n, shape, dt):
        out = []
        esz = 2 if dt == BF16 else 4
        sz = esz
        for s in shape:
            sz *= s
        for i in range(n):
            out.append(self.buf("%s%d" % (name, i), off + i * sz, shape, dt))
        return out


C_ID, C_MI, C_MS, C_TU, C_BD, C_EA, C_EB, C_ON, C_COS, C_SIN = (
    0, 128, 256, 384, 512, 640, 768, 896, 1024, 1280)
NCONST = 1536


def make_consts():
    c = np.zeros((128, NCONST), np.float32)
    s = np.arange(128)[:, None]
    j = np.arange(128)[None, :]
    same = (s // 64) == (j // 64)
    c[:, C_ID:C_ID + 128] = (s == j)
    c[:, C_MI:C_MI + 128] = (s <= j) & same
    c[:, C_MS:C_MS + 128] = (s < j) & same
    c[:, C_TU:C_TU + 128] = (s <= j)
    c[:, C_BD:C_BD + 128] = same
    c[:, C_EA:C_EA + 128] = (s < 64)
    c[:, C_EB:C_EB + 128] = (s >= 64)
    c[:, C_ON:C_ON + 128] = 1.0
    inv_freq = 500000.0 ** (-np.arange(16, dtype=np.float64) * (2.0 / 32))
    pos = np.arange(T, dtype=np.float64)
    ang = pos[:, None] * inv_freq[None, :]
    cos = np.cos(ang).astype(np.float32).reshape(NT, 128, 16).transpose(1, 0, 2).reshape(128, NT * 16)
    sin = np.sin(ang).astype(np.float32).reshape(NT, 128, 16).transpose(1, 0, 2).reshape(128, NT * 16)
    c[:, C_COS:C_COS + 256] = cos
    c[:, C_SIN:C_SIN + 256] = sin
    return c


ARENA_BYTES = 204800
O_CONST = 0
O_U = 8192
O_G = 40960
O_S = 122880
O_X = 126976
O_T = 184320


def build(nseq=2, taps=(), stop_after=None):
    nc = bass.Bass("TRN2", target_bir_lowering=False)

    def din(name, shape, dt=F32):
        return nc.dram_tensor(name, list(shape), dt, kind="ExternalInput").ap()

    x_d = din("x", [nseq, T, D])
    p_d = din("p", [nseq, T, 256])
    w_in_d = din("w_in", [D, INP])
    conv_w_d = din("conv_w", [4, 1536])
    a_log_d = din("a_log", [4])
    dt_bias_d = din("dt_bias", [4])
    gnw_d = din("gdn_norm_w", [128])
    w_out_d = din("w_out", [D, D])
    n_pre_d = din("attn_pre_norm", [D])
    n_post_d = din("attn_post_norm", [D])
    m_pre_d = din("mlp_pre_norm", [D])
    m_post_d = din("mlp_post_norm", [D])
    w_up_d = din("w_up", [D, DFF])
    w_down_d = din("w_down", [DFF, D])
    w_ple_d = din("w_ple", [256, D])
    w_gate_d = din("w_ple_gate", [D, D])
    consts_d = din("consts", [128, NCONST])
    out_d = nc.dram_tensor("out", [nseq, T, D], F32, kind="ExternalOutput").ap()

    def dint(name, shape):
        return nc.dram_tensor(name, list(shape), BF16, kind="Internal").ap()

    wb_in = dint("wb_in", [D, INP])
    wb_out = dint("wb_out", [2, 128, 8, 512])
    wb_up = dint("wb_up", [8, 128, 8, 512])
    wb_down = dint("wb_down", [4, 128, 32, 256])
    wb_gate = dint("wb_gate", [2, 128, 8, 512])
    wb_ple = dint("wb_ple", [128, 2, D])

    tap_out = {}

    with ExitStack() as st:
        P = Prog(nc, st)
        arena_t = st.enter_context(nc.sbuf_tensor("arena", [128, ARENA_BYTES // 4], F32))
        AR = Arena(arena_t[:, :], ARENA_BYTES)
        psum_t = st.enter_context(nc.psum_tensor("psum", [128, 4096], F32))
        PS = [Buf(psum_t[:, i * 512:(i + 1) * 512], "ps%d" % i, excl=True) for i in range(8)]

        def psbf(i):
            return PS[i].ap.bitcast(BF16)

        def ACT(out, in_, func, reads, writes, **kw):
            P.op("act", lambda e: e.activation(out=out, in_=in_, func=func, **kw), reads, writes)

        def TT(out, in0, in1, op, reads, writes, eng="dve"):
            P.op(eng, lambda e: e.tensor_tensor(out=out, in0=in0, in1=in1, op=op), reads, writes)

        def TS(out, in0, s1, s2, op0, op1, reads, writes, eng="dve"):
            if op1 is None:
                P.op(eng, lambda e: e.tensor_scalar(out=out, in0=in0, scalar1=s1, scalar2=None, op0=op0), reads, writes)
            else:
                P.op(eng, lambda e: e.tensor_scalar(out=out, in0=in0, scalar1=s1, scalar2=s2, op0=op0, op1=op1), reads, writes)

        def STT(out, in0, scalar, in1, op0, op1, reads, writes):
            P.op("dve", lambda e: e.scalar_tensor_tensor(out=out, in0=in0, scalar=scalar, in1=in1, op0=op0, op1=op1), reads, writes)

        def CP(eng, out, in_, reads, writes):
            if eng == "act":
                P.op("act", lambda e: e.copy(out=out, in_=in_), reads, writes)
            else:
                P.op(eng, lambda e: e.tensor_copy(out, in_), reads, writes)

        def MM(out, lhsT, rhs, start, stop, reads, writes, inc, sgc=False):
            if sgc:
                P.op("pe", lambda e: e.matmul(out, lhsT=lhsT, rhs=rhs, start=start, stop=stop, skip_group_check=True), reads, writes, inc=inc)
            else:
                P.op("pe", lambda e: e.matmul(out, lhsT=lhsT, rhs=rhs, start=start, stop=stop), reads, writes, inc=inc)

        def TR(out, in_, ident, reads, writes, inc):
            P.op("pe", lambda e: e.transpose(out, in_, ident), reads, writes, inc=inc)

        def bc_last(ap2, n):
            return ap2.unsqueeze(2).to_broadcast([ap2.shape[0], ap2.shape[1], n])

        def bc_mid(ap2, n):
            return ap2.unsqueeze(1).to_broadcast([ap2.shape[0], n, ap2.shape[1]])

        def tap(name, buf, ap=None, deps=()):
            if name not in taps:
                return
            ap = buf.ap if ap is None else ap
            shp = list(ap.shape)
            dt = ap.dtype
            d = nc.dram_tensor("tap_" + name, shp, dt, kind="ExternalOutput").ap()
            db = Buf(d, "tap_" + name)
            P.dma("sp", d, ap, reads=[buf] + list(deps), writes=[db])
            tap_out[name] = db

        cst = AR.buf("cst", O_CONST, [NCONST], F32)
        identb = AR.buf("identb", 6144, [128], BF16)
        onesb = AR.buf("onesb", 6400, [128], BF16)
        eps_t = AR.buf("eps", 6656, [1], F32)
        one_t = AR.buf("one", 6660, [1], F32)
        lnqs_t = AR.buf("lnqs", 6664, [1], F32)
        prmT = AR.buf("prmT", 6672, [65], F32)
        dtb_c = AR.buf("dtb", 6944, [4], F32)
        nA_c = AR.buf("nA", 6960, [4], F32)
        prm = AR.buf("prm", O_T, [128], F32)
        P.dma("sp", cst[:], consts_d, writes=[cst])
        ident_f = cst[:, C_ID:C_ID + 128]
        maskI = cst[:, C_MI:C_MI + 128]
        maskS = cst[:, C_MS:C_MS + 128]
        triU = cst[:, C_TU:C_TU + 128]
        bdones = cst[:, C_BD:C_BD + 128]
        Ea = cst[:, C_EA:C_EA + 128]
        Eb = cst[:, C_EB:C_EB + 128]
        ones_f = cst[:, C_ON:C_ON + 128]
        cos_t = cst[:, C_COS:C_COS + 256].rearrange("p (t f) -> p t f", t=NT)
        sin_t = cst[:, C_SIN:C_SIN + 256].rearrange("p (t f) -> p t f", t=NT)
        prm_parts = [Buf(prm[0:8, :], "prm0", inherits=[prm]), Buf(prm[8:16, :], "prm1", inherits=[prm]),
                     Buf(prm[16:64, :], "prm2", inherits=[prm]), Buf(prm[64:65, :], "prm3", inherits=[prm])]
        for b_ in prm_parts:
            AR.track(O_T, 512, b_)
        P.dma("sp", prm[0:8, :], n_pre_d.rearrange("(k p) -> k p", p=128), writes=[prm_parts[0]])
        P.dma("sp", prm[8:16, :], m_pre_d.rearrange("(k p) -> k p", p=128), writes=[prm_parts[1]])
        P.dma("sp", prm[16:64, :], conv_w_d.rearrange("j (c p) -> (j c) p", p=128), writes=[prm_parts[2]])
        P.dma("sp", prm[64:65, :], gnw_d.rearrange("(o p) -> o p", o=1), writes=[prm_parts[3]])
        P.dma("sp", dtb_c[:], dt_bias_d.partition_broadcast(128), writes=[dtb_c])
        P.dma("sp", nA_c[:], a_log_d.partition_broadcast(128), writes=[nA_c])
        P.op("dve", lambda e: e.tensor_copy(identb[:], ident_f), [cst], [identb])
        P.op("dve", lambda e: e.tensor_copy(onesb[:], ones_f), [cst], [onesb])
        P.op("dve", lambda e: e.memset(eps_t[:], EPS), (), [eps_t])
        P.op("dve", lambda e: e.memset(one_t[:], 1.0), (), [one_t])
        P.op("dve", lambda e: e.memset(lnqs_t[:], float(np.log(128.0 ** -0.5))), (), [lnqs_t])
        TR(PS[7][:, 0:65], prm[0:65, :], cst[0:65, C_ID:C_ID + 65], prm_parts + [cst], [PS[7]], True)
        CP("dve", prmT[:], PS[7][:, 0:65], [PS[7]], [prmT])
        wpre_c = Buf(prmT[:, 0:8], "wpre", inherits=[prmT])
        wmlp_c = Buf(prmT[:, 8:16], "wmlp", inherits=[prmT])
        cw_c = Buf(prmT[:, 16:64].rearrange("p (j c) -> p c j", j=4), "cw", inherits=[prmT])
        gnw_c = Buf(prmT[:, 64:65], "gnw", inherits=[prmT])

        def conv_w(dst, src, rows, rb, dep=()):
            bl = []
            for r0 in range(0, rows, rb):
                b = Buf(dst[r0:r0 + rb, :], "wb")
                P.dma("pool", dst[r0:r0 + rb, :], src[r0:r0 + rb, :], reads=list(dep), writes=[b])
                bl.append(b)
            return bl

        WB_IN = conv_w(wb_in, w_in_d, D, 256, [cst, prmT, dtb_c, nA_c])
        WB = {}

        def conv_t(dst, src, dep):
            b = Buf(dst, "wb")
            P.dma("pool", dst, src, reads=list(dep), writes=[b])
            return b

        def convert_rest(dep):
            WB["out"] = [conv_t(wb_out[hf], w_out_d[:, hf * 512:(hf + 1) * 512].rearrange("(k p) c -> p k c", p=128), dep) for hf in range(2)]
            WB["up"] = [conv_t(wb_up[cg], w_up_d[:, cg * 512:(cg + 1) * 512].rearrange("(k p) c -> p k c", p=128), dep) for cg in range(8)]
            WB["down"] = [conv_t(wb_down[q], w_down_d[:, q * 256:(q + 1) * 256].rearrange("(j p) c -> p j c", p=128), dep) for q in range(4)]
            WB["gate"] = [conv_t(wb_gate[hf], w_gate_d[:, hf * 512:(hf + 1) * 512].rearrange("(k p) c -> p k c", p=128), dep) for hf in range(2)]
            WB["ple"] = [conv_t(wb_ple, w_ple_d.rearrange("(k p) c -> p k c", p=128), dep)]

        def wview(wb, c0, ncols):
            return wb[:, c0:c0 + ncols].rearrange("(k p) c -> p k c", p=128)

        outbufs = []

        for seq in range(nseq):
            uT_all, _ = AR.view(O_U, [8, T], BF16)
            uT_reg = AR.buf("uTreg", O_U, [8, T], BF16)
            uT = [Buf(uT_all[:, :, t * 128:(t + 1) * 128], "uT%d" % t, inherits=[uT_reg]) for t in range(NT)]
            for b_ in uT:
                AR.track(O_U, 8 * T * 2, b_)
            xt = AR.bufs("xt", O_T, 3, [D], F32)
            xn = AR.bufs("xn", O_T + 12288, 2, [D], BF16)
            junk = AR.buf("junk", O_T + 16384, [D], BF16)
            ssA = AR.bufs("ssA", O_T + 18432, 3, [1], F32)

            def a_load(t):
                P.dma("sp", xt[t % 3][:], x_d[seq, t * 128:(t + 1) * 128, :], writes=[xt[t % 3]])

            def a_stage1(t):
                b3, b = t % 3, t % 2
                P.op("dve", lambda e: e.scalar_tensor_tensor(out=junk[:], in0=xt[b3][:], scalar=1.0, in1=xt[b3][:], op0=ALU.mult,
                                                                op1=ALU.mult, accum_out=ssA[b3][:]), [xt[b3]], [junk, ssA[b3]])
                ACT(ssA[b3][:], ssA[b3][:], AF.Sqrt, [ssA[b3], eps_t], [ssA[b3]], scale=1.0 / D, bias=eps_t[:])
                P.op("dve", lambda e: e.reciprocal(ssA[b3][:], ssA[b3][:]), [ssA[b3]], [ssA[b3]])
                ACT(xn[b][:], xt[b3][:], AF.Copy, [xt[b3], ssA[b3]], [xn[b]], scale=ssA[b3][:])

            def a_stage2(t):
                b, pb = t % 2, t % 2
                for k in range(8):
                    TR(psbf(pb)[:, k * 128:(k + 1) * 128], xn[b][:, k * 128:(k + 1) * 128], identb[:],
                       [xn[b], identb], [PS[pb]], inc=(k == 7))
                TT(uT[t][:], psbf(pb).rearrange("p (k c) -> p k c", k=8), bc_last(wpre_c[:], 128), ALU.mult,
                   [PS[pb], wpre_c], [uT[t]])

            a_load(0)
            a_load(1)
            for t in range(NT + 1):
                if t + 2 < NT:
                    a_load(t + 2)
                if t < NT:
                    a_stage1(t)
                if t >= 1:
                    a_stage2(t - 1)
            if seq == 0:
                convert_rest([uT[NT - 1]])
                tap("uT", uT_reg, uT_all, deps=uT)
            if stop_after == "A":
                break

            qT_all, _ = AR.view(O_G, [4, T], BF16)
            kT_all, _ = AR.view(O_G + 16384, [4, T], BF16)
            qT = AR.buf("qT", O_G, [4, T], BF16)
            kT = AR.buf("kT", O_G + 16384, [4, T], BF16)
            Vtok = AR.buf("Vtok", O_G + 32768, [NT, 512], BF16)
            Ktok = AR.buf("Ktok", O_G + 49152, [NT, 512], BF16)
            Zs = AR.buf("Zs", O_G + 65536, [NT, 512], BF16)
            beta = AR.buf("beta", O_S, [NT, 4], F32)
            gstep = AR.buf("gstep", O_S + 256, [NT, 4], F32)
            gcum = AR.buf("gcum", O_S + 512, [NT, 4], F32)
            eg = AR.buf("eg", O_S + 768, [NT, 4], F32)
            egl = AR.buf("egl", O_S + 1024, [NT, 4], F32)
            EGL = AR.buf("EGL", O_S + 1280, [2, NT, 4], F32)
            gbga = AR.buf("gbga", O_S + 1792, [NT, 8], F32)
            tmpg = AR.buf("tmpg", O_S + 2304, [NT, 4], F32)
            stage = AR.bufs("stage", O_X, 2, [2052], F32)
            acc_l = [AR.buf("acc", O_X + 16416, [T], F32), AR.buf("acc2", O_T, [T], F32)]
            sil_l = [AR.buf("sil", O_X + 24608, [T], F32), AR.buf("sil2", O_T + 8192, [T], F32)]
            sq_l = [AR.buf("sq", O_X + 32800, [T], BF16), AR.buf("sq2", O_T + 16384, [T], BF16)]
            wg = AR.bufs("wg", O_X + 36896, 2, [8, 512], BF16)
            wsm = AR.buf("wsm", O_X + 53280, [8, 8], BF16)
            for sgb in stage:
                P.op("dve", lambda e, sgb=sgb: e.memset(sgb[:, 0:4], 0.0), (), [sgb])

            P.dma("sp", wg[0][:], wview(wb_in, 1536, 512), reads=WB_IN, writes=[wg[0]])
            P.dma("sp", wsm[:], wview(wb_in, 2048, 8), reads=WB_IN, writes=[wsm])
            for t in range(NT):
                pb = t % 2
                for k in range(8):
                    MM(PS[pb][:, :], uT[t][:, k, :], wg[0][:, k, :], k == 0, k == 7, [uT[t], wg[0]], [PS[pb]], k == 7)
                ACT(Zs[:, t, :], PS[pb][:, :], AF.Silu, [PS[pb]], [Zs])
                for k in range(8):
                    MM(PS[2][:, t * 8:(t + 1) * 8], uT[t][:, k, :], wsm[:, k, :], k == 0, k == 7, [uT[t], wsm], [PS[2]],
                       (k == 7 and t == NT - 1))
            CP("dve", gbga[:], PS[2][:, 0:NT * 8].rearrange("p (t c) -> p t c", t=NT), [PS[2]], [gbga])
            if seq == 0:
                ACT(nA_c[:], nA_c[:], AF.Exp, [nA_c], [nA_c])
                TS(nA_c[:], nA_c[:], -1.0, None, ALU.mult, None, [nA_c], [nA_c])
            ACT(beta[:], gbga[:, :, 0:4], AF.Sigmoid, [gbga], [beta])
            TT(tmpg[:], gbga[:, :, 4:8], bc_mid(dtb_c[:], NT), ALU.add, [gbga, dtb_c], [tmpg])
            ACT(tmpg[:], tmpg[:], AF.Exp, [tmpg], [tmpg])
            ACT(tmpg[:], tmpg[:], AF.Ln, [tmpg, one_t], [tmpg], bias=one_t[:])
            TT(gstep[:], tmpg[:], bc_mid(nA_c[:], NT), ALU.mult, [tmpg, nA_c], [gstep])
            for i, m in enumerate((maskI, bdones, Ea, Eb)):
                MM(PS[3][:, i * 64:(i + 1) * 64], m, gstep[:].rearrange("p t c -> p (t c)"), True, True, [cst, gstep], [PS[3]], i == 3)
            CP("dve", gcum[:], PS[3][:, 0:64].rearrange("p (t c) -> p t c", t=NT), [PS[3]], [gcum])
            ACT(eg[:], gcum[:], AF.Exp, [gcum], [eg])
            TT(egl[:], PS[3][:, 64:128].rearrange("p (t c) -> p t c", t=NT), gcum[:], ALU.subtract, [PS[3], gcum], [egl])
            ACT(egl[:], egl[:], AF.Exp, [egl], [egl])
            ACT(EGL[:], PS[3][:, 128:256].rearrange("p (a t c) -> p a t c", a=2, t=NT), AF.Exp, [PS[3]], [EGL])
            if seq == 0:
                tap("beta", beta); tap("gcum", gcum); tap("Zs", Zs); tap("EGL", EGL); tap("egl", egl)

            def b_in(c):
                grp, h = c // 4, c % 4
                wgi = (grp + 1) % 2
                if h == 0:
                    P.dma("sp", wg[wgi][:], wview(wb_in, grp * 512, 512), reads=WB_IN, writes=[wg[wgi]])
                sg = stage[c % 2]
                for tg in range(4):
                    pb = tg % 2
                    for k in range(8):
                        MM(PS[pb][:, :], wg[wgi][:, k, h * 128:(h + 1) * 128], uT_all[:, k, tg * 512:(tg + 1) * 512],
                           k == 0, k == 7, [wg[wgi]] + uT[tg * 4:tg * 4 + 4], [PS[pb]], k == 7)
                    CP("act", sg[:, 4 + tg * 512: 4 + (tg + 1) * 512], PS[pb][:, :], [PS[pb]], [sg])

            def b_conv(c):
                sg = stage[c % 2]
                acc = acc_l[c % 2]
                TS(acc[:], sg[:, 4:4 + T], cw_c[:, c, 3:4], None, ALU.mult, None, [sg, cw_c], [acc])
                for j in range(3):
                    STT(acc[:], sg[:, 1 + j:1 + j + T], cw_c[:, c, j:j + 1], acc[:], ALU.mult, ALU.add, [sg, cw_c, acc], [acc])

            def b_silu(c):
                grp = c // 4
                acc, sil, sq = acc_l[c % 2], sil_l[c % 2], sq_l[c % 2]
                if grp == 2:
                    ACT(sq[:], acc[:], AF.Silu, [acc], [sq])
                else:
                    ACT(sil[:], acc[:], AF.Silu, [acc], [sil])
                    TT(sq[:], sil[:], sil[:], ALU.mult, [sil], [sq], eng="pool")

            def b_fin(c):
                grp, h = c // 4, c % 4
                acc, sil, sq = acc_l[c % 2], sil_l[c % 2], sq_l[c % 2]
                if grp == 2:
                    for half in range(2):
                        for tt in range(8):
                            t = half * 8 + tt
                            TR(psbf(4 + half)[:, tt * 128:(tt + 1) * 128], sq[:, t * 128:(t + 1) * 128], identb[:],
                               [sq, identb], [PS[4 + half]], tt == 7)
                        CP("act", Vtok[:, half * 8:(half + 1) * 8, h * 128:(h + 1) * 128],
                           psbf(4 + half).rearrange("p (t c) -> p t c", t=8), [PS[4 + half]], [Vtok])
                    return
                dstT = qT if grp == 0 else kT
                for tg in range(4):
                    MM(PS[2 + (tg % 2)][:, :], onesb[:], sq[:, tg * 512:(tg + 1) * 512], True, True,
                       [onesb, sq], [PS[2 + (tg % 2)]], True)
                    ACT(acc[:, tg * 512:(tg + 1) * 512], PS[2 + (tg % 2)][:, :], AF.Ln, [PS[2 + (tg % 2)], eps_t], [acc], bias=eps_t[:])
                if grp == 0:
                    ACT(acc[:], acc[:], AF.Exp, [acc, lnqs_t], [acc], scale=-0.5, bias=lnqs_t[:])
                else:
                    ACT(acc[:], acc[:], AF.Exp, [acc], [acc], scale=-0.5)
                for half in range(2):
                    hsl = slice(half * 1024, (half + 1) * 1024)
                    TT(dstT[:, h, hsl], sil[:, hsl], acc[:, hsl], ALU.mult, [sil, acc], [dstT])
                if grp == 1:
                    for half in range(2):
                        for tt in range(8):
                            t = half * 8 + tt
                            TR(psbf(4 + half)[:, tt * 128:(tt + 1) * 128], kT[:, h, t * 128:(t + 1) * 128], identb[:],
                               [kT, identb], [PS[4 + half]], tt == 7)
                        CP("act", Ktok[:, half * 8:(half + 1) * 8, h * 128:(h + 1) * 128],
                           psbf(4 + half).rearrange("p (t c) -> p t c", t=8), [PS[4 + half]], [Ktok])

            b_in(0)
            b_in(1)
            b_conv(0)
            b_silu(0)
            for c in range(12):
                if c + 2 < 12:
                    b_in(c + 2)
                if c + 1 < 12:
                    b_conv(c + 1)
                b_fin(c)
                if c + 1 < 12:
                    b_silu(c + 1)
            if seq == 0:
                tap("qT", qT); tap("kT", kT); tap("Vtok", Vtok); tap("Ktok", Ktok)
            if stop_after == "B":
                break

            CT0 = O_X + 32768
            mix_all, _ = AR.view(O_X, [8, T], BF16)
            mix_reg = AR.buf("mixreg", O_X, [8, T], BF16)
            mixG = [Buf(mix_all[:, 0:4, t * 128:(t + 1) * 128], "mixG%d" % t, inherits=[mix_reg]) for t in range(NT)]
            mixM = [Buf(mix_all[:, 4:8, t * 128:(t + 1) * 128], "mixM%d" % t, inherits=[mix_reg]) for t in range(NT)]
            for b_ in mixG + mixM:
                AR.track(O_X, 8 * T * 2, b_)
            GM = AR.buf("GM", CT0, [4, 128], F32)
            DMi = AR.buf("DMi", CT0 + 4096, [4, 128], F32)
            nbM = AR.buf("nbM", CT0 + 6144, [4, 128], F32)
            tq = AR.buf("tq", CT0 + 8192, [4, 128], F32)
            Dsc2 = [AR.buf("Dsc", CT0 + 2048, [4, 128], F32), AR.buf("DscB", CT0 + 35840, [4, 128], F32)]
            EGr2 = [AR.buf("EGr", CT0 + 10240, [4, 128], F32), AR.buf("EGrB", CT0 + 37888, [4, 128], F32)]
            Qb = AR.bufs("Qb", CT0 + 12288, 2, [4, 128], BF16)
            Pb = AR.bufs("Pb", CT0 + 14336, 2, [4, 128], BF16)
            Rb = AR.bufs("Rb", CT0 + 16384, 2, [4, 128], BF16)
            ke = AR.buf("ke", CT0 + 18432, [4, 128], BF16)
            ATb2 = AR.bufs("AT", CT0 + 19456, 2, [4, 128], BF16)
            RF2 = AR.bufs("RF", CT0 + 21504, 2, [4, 128], BF16)
            kdec2 = AR.bufs("kdec", CT0 + 23552, 2, [4, 128], BF16)
            qeT2 = AR.bufs("qeT", CT0 + 25600, 2, [4, 128], BF16)
            qeB2 = AR.bufs("qeB", CT0 + 41216, 2, [4, 128], BF16)
            for b_ in qeT2 + qeB2:
                P.op("pool", lambda e, b_=b_: e.memset(b_[:], 0.0), (), [b_])
            nW2 = AR.bufs("nW", CT0 + 27648, 2, [4, 128], BF16)
            vnew = AR.buf("vnew", CT0 + 29696, [4, 128], BF16)
            S32 = AR.buf("S32", CT0 + 30720, [4, 128], F32)
            Sdec = AR.buf("Sdec", CT0 + 32768, [4, 128], F32)
            Sbf = AR.buf("Sbf", CT0 + 34816, [4, 128], BF16)
            sqo = tq
            og1 = DMi
            OGb = AR.buf("OG", CT0 + 39936, [4, 128], BF16)
            ssO = AR.buf("ssO", CT0 + 40960, [4], F32)

            def v4(bank, bf=False):
                a = psbf(bank)[:, 0:512] if bf else PS[bank].ap
                return a.rearrange("p (h c) -> p h c", h=4)

            P.op("dve", lambda e: e.memset(S32[:], 0.0), (), [S32])
            P.op("dve", lambda e: e.memset(Sbf[:], 0.0), (), [Sbf])

            def c_decay_units(t):
                Dsc, EGr = Dsc2[t % 2], EGr2[t % 2]

                def d0():
                    TT(GM[:], bc_mid(maskI, 4), bc_last(gstep[:, t, :], 128), ALU.mult, [cst, gstep], [GM], eng="pool")

                def d1():
                    MM(PS[0][:, :], ones_f, GM[:].rearrange("p h c -> p (h c)"), True, True, [cst, GM], [PS[0]], True)

                def d2():
                    TT(Dsc[:], v4(0), bc_last(gcum[:, t, :], 128), ALU.min, [PS[0], gcum], [Dsc])
                    TT(Dsc[:], Dsc[:], bc_last(gcum[:, t, :], 128), ALU.subtract, [Dsc, gcum], [Dsc])
                    ACT(EGr[:], v4(0), AF.Exp, [PS[0]], [EGr])

                def d3():
                    ACT(Dsc[:], Dsc[:], AF.Exp, [Dsc], [Dsc])
                return [d0, d1, d2, d3]

            def c_prep_units(t):
                tc = slice(t * 128, (t + 1) * 128)
                ATb, RF, kdec, qeT, nW = ATb2[t % 2], RF2[t % 2], kdec2[t % 2], qeT2[t % 2], nW2[t % 2]
                qeB = qeB2[t % 2]
                Dsc, EGr = Dsc2[t % 2], EGr2[t % 2]
                U = []

                def u0():
                    for h in range(4):
                        MM(PS[1][:, h * 128:(h + 1) * 128], kT[:, h, tc], kT[:, h, tc], True, True, [kT], [PS[1]], h == 3)
                    for h in range(4):
                        MM(PS[2][:, h * 128:(h + 1) * 128], kT[:, h, tc], qT[:, h, tc], True, True, [kT, qT], [PS[2]], h == 3)
                    TT(nbM[:], bc_mid(maskS, 4), bc_last(beta[:, t, :], 128), ALU.mult, [cst, beta], [nbM], eng="pool")
                    TT(ke[:], Ktok[:, t, :].rearrange("p (h c) -> p h c", h=4), bc_last(eg[:, t, :], 128), ALU.mult, [Ktok, eg], [ke], eng="pool")
                    TT(kdec[:], Ktok[:, t, :].rearrange("p (h c) -> p h c", h=4), bc_last(egl[:, t, :], 128), ALU.mult, [Ktok, egl], [kdec], eng="pool")
                U.append(u0)

                def u3():
                    TT(qeT[:, :, 0:64], qT[:, :, t * 128:t * 128 + 64], EGr[:, :, 0:64], ALU.mult, [qT, EGr], [qeT])
                    TT(qeB[:, :, 64:128], qT[:, :, t * 128 + 64:t * 128 + 128], EGr[:, :, 64:128], ALU.mult, [qT, EGr], [qeB])
                U.append(u3)

                def u4():
                    TT(tq[:], v4(1), Dsc[:], ALU.mult, [PS[1], Dsc], [tq])
                    STT(Qb[0][:], tq[:], -1.0, nbM[:], ALU.mult, ALU.mult, [tq, nbM], [Qb[0]])
                    TT(DMi[:], Dsc[:], bc_mid(maskI, 4), ALU.mult, [Dsc, cst], [DMi], eng="pool")
                U.append(u4)

                def u5():
                    for h in range(4):
                        TR(psbf(3)[:, h * 128:(h + 1) * 128], Qb[0][:, h, :], identb[:], [Qb[0], identb], [PS[3]], h == 3)
                    TT(Rb[0][:], Qb[0][:], bc_mid(ident_f, 4), ALU.add, [Qb[0], cst], [Rb[0]])
                    TT(ATb[:], v4(2), DMi[:], ALU.mult, [PS[2], DMi], [ATb])
                U.append(u5)

                def u6():
                    CP("act", Pb[0][:], v4(3, True), [PS[3]], [Pb[0]])
                U.append(u6)

                def mk_sq(lv):
                    def f():
                        cur, nxt = lv % 2, 1 - lv % 2
                        for h in range(4):
                            MM(PS[1][:, h * 128:(h + 1) * 128], Qb[cur][:, h, :], Pb[cur][:, h, :], True, True, [Qb[cur], Pb[cur]], [PS[1]], h == 3)
                        if lv < 4:
                            for h in range(4):
                                MM(PS[2][:, h * 128:(h + 1) * 128], Pb[cur][:, h, :], Qb[cur][:, h, :], True, True, [Qb[cur], Pb[cur]], [PS[2]], h == 3)
                        if lv > 0:
                            pc, pn = (lv - 1) % 2, 1 - (lv - 1) % 2
                            for h in range(4):
                                MM(PS[3][:, h * 128:(h + 1) * 128], Pb[pn][:, h, :], Rb[pc][:, h, :], True, False, [Pb[pn], Rb[pc]], [PS[3]], False)
                                MM(PS[3][:, h * 128:(h + 1) * 128], identb[:], Rb[pc][:, h, :], False, True, [identb, Rb[pc]], [PS[3]], h == 3)
                    return f

                def mk_ev(lv):
                    def f():
                        cur, nxt = lv % 2, 1 - lv % 2
                        CP("act", Pb[nxt][:], v4(1), [PS[1]], [Pb[nxt]])
                        if lv < 4:
                            CP("dve", Qb[nxt][:], v4(2), [PS[2]], [Qb[nxt]])
                        if lv > 0:
                            pn = 1 - (lv - 1) % 2
                            CP("dve" if lv == 4 else "act", Rb[pn][:], v4(3), [PS[3]], [Rb[pn]])
                    return f

                for lv in range(5):
                    U.append(mk_sq(lv))
                    U.append(mk_ev(lv))

                def u_rl():
                    for h in range(4):
                        MM(PS[3][:, h * 128:(h + 1) * 128], Pb[1][:, h, :], Rb[0][:, h, :], True, False, [Pb[1], Rb[0]], [PS[3]], False)
                        MM(PS[3][:, h * 128:(h + 1) * 128], identb[:], Rb[0][:, h, :], False, True, [identb, Rb[0]], [PS[3]], h == 3)
                U.append(u_rl)

                def u_rf():
                    CP("dve", RF[:], v4(3), [PS[3]], [RF])
                U.append(u_rf)

                def u_w():
                    for h in range(4):
                        MM(PS[0][:, h * 128:(h + 1) * 128], ke[:, h, :], RF[:, h, :], True, True, [ke, RF], [PS[0]], h == 3)
                U.append(u_w)

                def u_we():
                    ACT(nW[:], v4(0), AF.Copy, [PS[0]], [nW], scale=-1.0)
                U.append(u_we)
                return U

            def c_seq_units(t):
                ATb, RF, kdec, qeT, nW = ATb2[t % 2], RF2[t % 2], kdec2[t % 2], qeT2[t % 2], nW2[t % 2]
                qeB = qeB2[t % 2]
                U = []
                for ci in range(2):
                    rr = slice(ci * 64, ci * 64 + 64)

                    def a(rr=rr):
                        for h in range(4):
                            hs = slice(h * 128, (h + 1) * 128)
                            MM(PS[4][:, hs], RF[rr, h, :], Vtok[rr, t, hs], True, False, [RF, Vtok], [PS[4]], False)
                            MM(PS[4][:, hs], nW[:, h, :], Sbf[:, h, :], False, True, [nW, Sbf], [PS[4]], h == 3)

                    def b(rr=rr, ci=ci):
                        TT(vnew[rr, :, :], PS[4][rr, :].rearrange("p (h c) -> p h c", h=4), bc_last(beta[rr, t, :], 128), ALU.mult,
                           [PS[4], beta], [vnew])
                        TT(Sdec[:], S32[:], bc_last(EGL[:, ci, t, :], 128), ALU.mult, [S32, EGL], [Sdec])

                    def c(rr=rr, ci=ci):
                        for h in range(4):
                            hs = slice(h * 128, (h + 1) * 128)
                            MM(PS[5][:, hs], kdec[rr, h, :], vnew[rr, h, :], True, True, [kdec, vnew], [PS[5]], h == 3)
                        qe = qeT if ci == 0 else qeB
                        for h in range(4):
                            hs = slice(h * 128, (h + 1) * 128)
                            MM(PS[6][:, hs], qe[:, h, :], Sbf[:, h, :], ci == 0 and h == 0, False, [qe, Sbf], [PS[6]], False, sgc=True)
                            MM(PS[6][:, hs], ATb[rr, h, :], vnew[rr, h, :], False, ci == 1, [ATb, vnew], [PS[6]], h == 3, sgc=True)

                    def d():
                        TT(Sbf[:], Sdec[:], v4(5), ALU.add, [Sdec, PS[5]], [Sbf])
                        TT(S32[:], Sdec[:], v4(5), ALU.add, [Sdec, PS[5]], [S32])
                    U += [a, b, c, d]

                def e():
                    ACT(sqo[:], v4(6), AF.Square, [PS[6]], [sqo])
                U.append(e)

                def f():
                    P.op("dve", lambda e_: e_.tensor_reduce(out=ssO[:], in_=sqo[:], axis=AX.X, op=ALU.add), [sqo], [ssO])
                U.append(f)

                def g_():
                    ACT(ssO[:], ssO[:], AF.Ln, [ssO, eps_t], [ssO], scale=1.0 / 128, bias=eps_t[:])
                    ACT(ssO[:], ssO[:], AF.Exp, [ssO], [ssO], scale=-0.5)
                U.append(g_)

                def h_():
                    TT(og1[:], v4(6), bc_last(ssO[:], 128), ALU.mult, [PS[6], ssO], [og1])
                    TT(OGb[:], og1[:], Zs[:, t, :].rearrange("p (h c) -> p h c", h=4), ALU.mult, [og1, Zs], [OGb], eng="pool")
                U.append(h_)

                def i_():
                    for h in range(4):
                        TR(psbf(7)[:, h * 128:(h + 1) * 128], OGb[:, h, :], identb[:], [OGb, identb], [PS[7]], h == 3)
                U.append(i_)

                def j_():
                    CP("act", mixG[t][:], v4(7, True), [PS[7]], [mixG[t]])
                U.append(j_)
                return U

            for f_ in c_decay_units(0):
                f_()
            for f_ in c_prep_units(0):
                f_()
            for f_ in c_decay_units(1):
                f_()
            DEC0 = 12
            for t in range(NT):
                pu = c_prep_units(t + 1) if t + 1 < NT else []
                su = c_seq_units(t)
                du = c_decay_units(t + 2) if t + 2 < NT else []
                n_ = max(len(pu), len(su), DEC0 + len(du))
                for i_u in range(n_):
                    if i_u < len(su):
                        su[i_u]()
                    if i_u < len(pu):
                        pu[i_u]()
                    if DEC0 <= i_u < DEC0 + len(du):
                        du[i_u - DEC0]()
            if stop_after == "C":
                if seq == 0:
                    tap("mixT", mix_reg, mix_all, deps=mixG)
                break

            mqT = AR.buf("mqT", O_G, [4, T], BF16)
            mkT = AR.buf("mkT", O_G + 16384, [4, T], BF16)
            Vp = AR.buf("Vp", O_G + 32768, [NT, 4, 130], BF16)
            sel = AR.buf("sel", O_G + 49408, [NT, 4, 8], F32)
            kmean = AR.buf("kmean", O_G + 51456, [4, 8], F32)
            ksum = AR.buf("ksum", O_G + 51584, [NT, 4], F32)
            ksum2 = AR.buf("ksum2", O_G + 51840, [8, 4], F32)
            wgm = AR.bufs("wgm", CT0, 2, [8, 512], BF16)
            stg = AR.bufs("stg", CT0 + 16384, 2, [4, 128], F32) + [AR.buf("stg2", CT0 + 28672, [4, 128], F32)]
            qT32 = AR.buf("qT32", CT0 + 20480, [4, 128], F32)
            qT32l = [qT32, AR.buf("qT32b", CT0 + 34816, [4, 128], F32)]
            gate_sb = AR.buf("gate_sb", CT0 + 22528, [4, 8], F32)
            top8 = AR.buf("top8", CT0 + 22656, [4, 8], F32)
            rt = AR.bufs("rt", CT0 + 22784, 4, [4, 16], F32)
            PT = AR.bufs("PT", CT0 + 30720, 6, [256], BF16)
            PTall, _ = AR.view(CT0 + 30720, [3, 512], BF16)
            Oacc = AR.buf("Oacc", CT0 + 25856, [2, 132], F32)
            rden = AR.buf("rden", CT0 + 26912, [2], F32)
            omb = AR.buf("omb", CT0 + 26920, [2, 128], BF16)

            def rotary(sb, t):
                x1 = sb[:, :, 0:16]
                x2 = sb[:, :, 16:32]
                cs = bc_mid(cos_t[:, t, :], 4)
                sn = bc_mid(sin_t[:, t, :], 4)
                TT(rt[0][:], x1, cs, ALU.mult, [sb, cst], [rt[0]])
                TT(rt[1][:], x2, sn, ALU.mult, [sb, cst], [rt[1]])
                TT(rt[2][:], x2, cs, ALU.mult, [sb, cst], [rt[2]])
                TT(rt[3][:], x1, sn, ALU.mult, [sb, cst], [rt[3]])
                TT(x1, rt[0][:], rt[1][:], ALU.subtract, [rt[0], rt[1]], [sb])
                TT(x2, rt[2][:], rt[3][:], ALU.add, [rt[2], rt[3]], [sb])

            P.dma("sp", wgm[0][:], wview(wb_in, 2568, 512), reads=WB_IN, writes=[wgm[0]])
            P.dma("sp", wgm[1][:], wview(wb_in, 3080, 512), reads=WB_IN, writes=[wgm[1]])
            P.op("dve", lambda e: e.memset(Vp[:, :, :, 128:130], 1.0), (), [Vp])

            def d_mm(t, w, sb3):
                b = t % 2
                for k in range(8):
                    MM(PS[b][:, :], uT[t][:, k, :], w[:, k, :], k == 0, k == 7, [uT[t], w], [PS[b]], k == 7)
                CP("act", sb3[:], v4(b), [PS[b]], [sb3])
                rotary(sb3, t)

            def d_ktr(t, sb3):
                b = t % 2
                for h in range(4):
                    TR(PS[2 + b][:, h * 128:(h + 1) * 128], sb3[:, h, :], ident_f, [sb3, cst], [PS[2 + b]], h == 3)
                CP("act", mkT[:, :, t * 128:(t + 1) * 128], v4(2 + b), [PS[2 + b]], [mkT])
                P.op("dve", lambda e: e.tensor_reduce(out=ksum[:, t, :], in_=v4(2 + b), axis=AX.X, op=ALU.add),
                     [PS[2 + b]], [ksum])

            def d_qtr(t, sb3):
                b = t % 2
                qb = t // 2
                for h in range(4):
                    TR(PS[2 + b][:, h * 128:(h + 1) * 128], sb3[:, h, :], ident_f, [sb3, cst], [PS[2 + b]], h == 3)
                ACT(mqT[:, :, t * 128:(t + 1) * 128], v4(2 + b), AF.Copy, [PS[2 + b]], [mqT], scale=float(128.0 ** -0.5))
                if qb >= 4:
                    CP("dve", qT32l[t % 2][:], v4(2 + b), [PS[2 + b]], [qT32l[t % 2]])

            def d_gate(t):
                qb = t // 2
                if qb < 4:
                    return
                q32 = qT32l[t % 2]
                for h in range(4):
                    MM(PS[4][:, h * 8:(h + 1) * 8], q32[:, h, :], kmean[:, h, :], True, True, [q32, kmean], [PS[4]], h == 3)
                CP("dve", gate_sb[:], PS[4][:, 0:32].rearrange("p (h n) -> p h n", h=4), [PS[4]], [gate_sb])
                P.op("dve", lambda e: e.memset(gate_sb[:, :, qb:8], -1e30), (), [gate_sb])
                for h in range(4):
                    P.op("dve", lambda e, h=h: e.max(out=top8[:, h, :], in_=gate_sb[:, h, :]), [gate_sb], [top8])
                TT(sel[:, t, :, :], gate_sb[:], bc_last(top8[:, :, 2], 8), ALU.is_ge, [gate_sb, top8], [sel])

            d_mm(0, wgm[0], stg[0])
            d_mm(1, wgm[0], stg[1])
            for t in range(NT):
                if t + 2 < NT:
                    d_mm(t + 2, wgm[0], stg[(t + 2) % 3])
                d_ktr(t, stg[t % 3])
            ks4 = ksum[:].rearrange("p (n two) h -> p n two h", two=2)
            TT(ksum2[:], ks4[:, :, 0, :], ks4[:, :, 1, :], ALU.add, [ksum], [ksum2])
            TS(kmean[:].rearrange("p h n -> p n h"), ksum2[:], 1.0 / 256, None, ALU.mult, None, [ksum2], [kmean])
            if stop_after == "D1":
                break
            for t in range(NT):
                b = t % 2
                for k in range(8):
                    MM(PS[b][:, :], uT[t][:, k, :], wgm[1][:, k, :], k == 0, k == 7, [uT[t], wgm[1]], [PS[b]], k == 7)
                CP("act", Vp[:, t, :, 0:128], v4(b), [PS[b]], [Vp])
            P.dma("sp", wgm[0][:], wview(wb_in, 2056, 512), reads=WB_IN, writes=[wgm[0]])
            d_mm(0, wgm[0], stg[0])
            d_mm(1, wgm[0], stg[1])
            for t in range(NT):
                if t + 2 < NT:
                    d_mm(t + 2, wgm[0], stg[(t + 2) % 3])
                d_qtr(t, stg[t % 3])
                if t >= 1:
                    d_gate(t - 1)
            d_gate(NT - 1)
            if seq == 0:
                tap("mqT", mqT); tap("mkT", mkT); tap("sel", sel); tap("Vp", Vp)
            if stop_after == "D2":
                break
            steps = []
            for h in range(4):
                for qb in range(8):
                    blocks = [qb] + list(range(qb))
                    for bi, n in enumerate(blocks):
                        steps.append((h, qb, n, bi == 0, bi == len(blocks) - 1))

            def att_st(i):
                h, qb, n, own, last = steps[i]
                r = i % 3
                bank = PS[r]
                sA = bank[:, 0:256]
                sB = bank[:, 256:512]
                pA, pB = PT[2 * r], PT[2 * r + 1]
                qc = slice(qb * 256, (qb + 1) * 256)
                k0 = n * 256
                if own:
                    MM(sA, mkT[:, h, k0:k0 + 128], mqT[:, h, qc], True, True, [mkT, mqT], [bank], False)
                    MM(sB[:, 0:128], mkT[:, h, k0 + 128:k0 + 256], mqT[:, h, qb * 256 + 128:qb * 256 + 256], True, True,
                       [mkT, mqT], [bank], True)
                    ACT(pA[:], sA, AF.Exp, [bank], [pA])
                    ACT(pB[:, 0:128], sB[:, 0:128], AF.Exp, [bank], [pB])
                    TT(pA[:, 0:128], pA[:, 0:128], triU, ALU.mult, [pA, cst], [pA], eng="pool")
                    TT(pB[:, 0:128], pB[:, 0:128], triU, ALU.mult, [pB, cst], [pB], eng="pool")
                else:
                    MM(sA, mkT[:, h, k0:k0 + 128], mqT[:, h, qc], True, True, [mkT, mqT], [bank], False)
                    MM(sB, mkT[:, h, k0 + 128:k0 + 256], mqT[:, h, qc], True, True, [mkT, mqT], [bank], True)
                    ACT(PTall[:, r, :], bank[:, :], AF.Exp, [bank], [pA, pB])

            def att_pv(i):
                h, qb, n, own, last = steps[i]
                r = i % 3
                pA, pB = PT[2 * r], PT[2 * r + 1]
                ob = 3 + r
                OV = PS[ob][:, 0:264].rearrange("p (q c) -> p q c", q=2)
                if own:
                    MM(OV[:, 0, 0:129], pA[:, 0:128], Vp[:, 2 * n, h, 0:129], True, True, [pA, Vp], [PS[ob]], False)
                    MM(OV[:, 1, 0:129], pA[:, 128:256], Vp[:, 2 * n, h, 0:129], True, False, [pA, Vp], [PS[ob]], False)
                    MM(OV[:, 1, 0:129], pB[:, 0:128], Vp[:, 2 * n + 1, h, 0:129], False, True, [pB, Vp], [PS[ob]], True)
                    CP("dve", Oacc[:, :, 0:129], OV[:, :, 0:129], [PS[ob]], [Oacc])
                else:
                    for qt in range(2):
                        MM(OV[:, qt, 0:129], pA[:, qt * 128:(qt + 1) * 128], Vp[:, 2 * n, h, 0:129], True, False, [pA, Vp], [PS[ob]], False)
                        MM(OV[:, qt, 0:129], pB[:, qt * 128:(qt + 1) * 128], Vp[:, 2 * n + 1, h, 0:129], False, True, [pB, Vp], [PS[ob]], qt == 1)
                    if qb >= 4:
                        for qt in range(2):
                            STT(Oacc[:, qt, 0:129], OV[:, qt, 0:129], sel[:, 2 * qb + qt, h, n:n + 1], Oacc[:, qt, 0:129],
                                ALU.mult, ALU.add, [PS[ob], sel, Oacc], [Oacc])
                    else:
                        TT(Oacc[:, :, 0:129], Oacc[:, :, 0:129], OV[:, :, 0:129], ALU.add, [Oacc, PS[ob]], [Oacc])
                if last:
                    ob_ = ombs[qb % 2]
                    P.op("dve", lambda e: e.reciprocal(rden[:], Oacc[:, :, 128]), [Oacc], [rden])
                    TT(ob_[:], Oacc[:, :, 0:128], bc_last(rden[:], 128), ALU.mult, [Oacc, rden], [ob_])
                    pending.append((i + 2, h, qb, ob_))

            def att_tail(i, force=False):
                while pending and (force or pending[0][0] <= i):
                    _, h, qb, ob_ = pending.pop(0)
                    tb = 6 + qb % 2
                    for qt in range(2):
                        TR(psbf(tb)[:, qt * 128:(qt + 1) * 128], ob_[:, qt, :], identb[:], [ob_, identb], [PS[tb]], qt == 1)
                    for qt in range(2):
                        CP("act", mixM[2 * qb + qt][:, h, :], psbf(tb)[:, qt * 128:(qt + 1) * 128], [PS[tb]], [mixM[2 * qb + qt]])

            ns = len(steps)
            pending = []
            ombs = [omb, AR.buf("ombB", CT0 + 33792, [2, 128], BF16)]
            att_st(0)
            att_st(1)
            for i in range(ns):
                if i + 2 < ns:
                    att_st(i + 2)
                att_pv(i)
                att_tail(i)
            att_tail(ns, force=True)
            if stop_after == "D":
                if seq == 0:
                    tap("mixT", mix_reg, mix_all, deps=mixG + mixM)
                break

            EB = O_U
            h1 = AR.bufs("h1", EB, 4, [D], F32)
            fsb = AR.bufs("fsb", EB + 16384, 4, [D], F32)
            nT_all, _ = AR.view(EB + 32768, [8, 512], BF16)
            nT_reg = AR.buf("nTreg", EB + 32768, [8, 512], BF16)
            hid_all, _ = AR.view(EB + 40960, [32, 512], BF16)
            hid_reg = AR.buf("hidreg", EB + 40960, [32, 512], BF16)
            wbuf = AR.bufs("wbuf", EB + 73728, 3, [8, 512], BF16)
            npost_b = AR.buf("npost_b", EB + 98304, [D], F32)
            mpost_b = AR.buf("mpost_b", EB + 102400, [D], F32)
            pT_all, _ = AR.view(EB + 106496, [2, 512], BF16)
            pT_reg = AR.buf("pTreg", EB + 106496, [2, 512], BF16)
            ssE = AR.bufs("ssE", EB + 108544, 4, [1], F32)
            pb16 = AR.buf("pb16", EB + 108576, [256], BF16)
            wdn = AR.bufs("wdn", CT0, 2, [32, 256], BF16)
            if seq == 0:
                for hf in range(2):
                    ftmp = AR.buf("ftmp%d" % hf, CT0 + 32768 + hf * 4096, [4, 512], BF16)
                    P.dma("pool", ftmp[:], wb_out[hf][:, 0:4, :], reads=[WB["out"][hf]], writes=[ftmp])
                    TS(ftmp[:], ftmp[:], gnw_c[:, 0:1], None, ALU.mult, None, [ftmp, gnw_c], [ftmp])
                    P.dma("pool", wb_out[hf][:, 0:4, :], ftmp[:], reads=[ftmp], writes=[WB["out"][hf]])
            xt2l = [AR.buf("xt2", CT0 + 32768, [D], F32), AR.buf("xt2b", CT0 + 36864, [D], F32)]
            junkE = AR.buf("junkE", EB + 113408, [D], BF16)
            ssF = AR.bufs("ssF", EB + 115456, 4, [1], F32)
            pb16l = [pb16, AR.buf("pb16b", EB + 115472, [256], BF16)]
            sqE = [AR.buf("sqE0", CT0 + 40960, [512], F32), AR.buf("sqE1", EB + 109312, [512], F32)]
            hb3 = [AR.buf("hb", CT0 + 43008, [D], BF16), AR.buf("hbB", EB + 111360, [D], BF16), AR.buf("hbC", EB + 115984, [D], BF16)]
            P.dma("sp", npost_b[:], n_post_d.partition_broadcast(128), writes=[npost_b])
            P.dma("sp", mpost_b[:], m_post_d.partition_broadcast(128), writes=[mpost_b])
            wrot = [0]

            def next_w():
                w = wbuf[wrot[0] % 3]
                wrot[0] += 1
                return w

            def rms_scale(src_ap, srcbuf, ssb, junkbuf):
                ACT(junkbuf[:], src_ap, AF.Square, [srcbuf], [junkbuf, ssb], accum_out=ssb[:])
                ACT(ssb[:], ssb[:], AF.Sqrt, [ssb, eps_t], [ssb], scale=1.0 / D, bias=eps_t[:])
                P.op("dve", lambda e: e.reciprocal(ssb[:], ssb[:]), [ssb], [ssb])

            nT = [Buf(nT_all[:, :, i * 128:(i + 1) * 128], "nT%d" % i, inherits=[nT_reg]) for i in range(4)]
            hid = [Buf(hid_all[:, j, :], "hid%d" % j, inherits=[hid_reg]) for j in range(32)]
            pT = [Buf(pT_all[:, :, i * 128:(i + 1) * 128], "pT%d" % i, inherits=[pT_reg]) for i in range(4)]
            for b_ in nT:
                AR.track(EB + 32768, 8192, b_)
            for b_ in hid:
                AR.track(EB + 40960, 32768, b_)
            for b_ in pT:
                AR.track(EB + 106496, 2048, b_)
            hidS = [[Buf(hid_all[:, j, ub * 256:(ub + 1) * 256], "hid%d_%d" % (ub, j), inherits=[hid[j]]) for j in range(32)]
                    for ub in range(2)]
            for ub in range(2):
                for b_ in hidS[ub]:
                    AR.track(EB + 40960, 32768, b_)

            sqH = [Buf(sqE[0][:, 0:256], "sqH0", inherits=[sqE[0]]), Buf(sqE[0][:, 256:512], "sqH1", inherits=[sqE[0]]),
                   Buf(sqE[1][:, 0:256], "sqH2", inherits=[sqE[1]]), Buf(sqE[1][:, 256:512], "sqH3", inherits=[sqE[1]])]
            AR.track(CT0 + 40960, 2048, sqH[0]); AR.track(CT0 + 40960, 2048, sqH[1])
            AR.track(EB + 109312, 2048, sqH[2]); AR.track(EB + 109312, 2048, sqH[3])

            def zip_units(lists):
                n_ = max(len(l) for l in lists)
                for ui in range(n_):
                    for l in lists:
                        if ui < len(l):
                            l[ui]()

            def e1b_units(t, i):
                hbb = hb3[i % 3]
                xt2 = xt2l[i % 2]
                tb = 2 if i % 2 == 0 else 7
                return [
                    lambda: (P.dma("pool", xt2[:], x_d[seq, t * 128:(t + 1) * 128, :], writes=[xt2]),
                             ACT(junkE[:], fsb[i][:], AF.Square, [fsb[i]], [junkE, ssE[i]], accum_out=ssE[i][:])),
                    lambda: ACT(ssE[i][:], ssE[i][:], AF.Sqrt, [ssE[i], eps_t], [ssE[i]], scale=1.0 / D, bias=eps_t[:]),
                    lambda: P.op("dve", lambda e: e.reciprocal(ssE[i][:], ssE[i][:]), [ssE[i]], [ssE[i]]),
                    lambda: STT(fsb[i][:], fsb[i][:], ssE[i][:], npost_b[:], ALU.mult, ALU.mult, [fsb[i], ssE[i], npost_b], [fsb[i]]),
                    lambda: TT(h1[i][:], fsb[i][:], xt2[:], ALU.add, [fsb[i], xt2], [h1[i]]),
                    lambda: ACT(junkE[:], h1[i][:], AF.Square, [h1[i]], [junkE, ssF[i]], accum_out=ssF[i][:]),
                    lambda: ACT(ssF[i][:], ssF[i][:], AF.Sqrt, [ssF[i], eps_t], [ssF[i]], scale=1.0 / D, bias=eps_t[:]),
                    lambda: P.op("dve", lambda e: e.reciprocal(ssF[i][:], ssF[i][:]), [ssF[i]], [ssF[i]]),
                    lambda: ACT(hbb[:], h1[i][:], AF.Copy, [h1[i], ssF[i]], [hbb], scale=ssF[i][:]),
                    lambda: [TR(psbf(tb)[:, k * 128:(k + 1) * 128], hbb[:, k * 128:(k + 1) * 128], identb[:], [hbb, identb], [PS[tb]], k == 7)
                             for k in range(8)],
                    lambda: TT(nT[i][:], psbf(tb).rearrange("p (k c) -> p k c", k=8), bc_last(wmlp_c[:], 128), ALU.mult,
                               [PS[tb], wmlp_c], [nT[i]]),
                ]

            def e4_units(t, i):
                hbb = hb3[i % 3]
                xt2 = xt2l[i % 2]
                pbb = pb16l[i % 2]
                tb = 2 if i % 2 == 0 else 7
                return [
                    lambda: (P.dma("pool", xt2[:, 0:256], p_d[seq, t * 128:(t + 1) * 128, :], writes=[xt2]),
                             ACT(junkE[:], fsb[i][:], AF.Square, [fsb[i]], [junkE, ssE[i]], accum_out=ssE[i][:])),
                    lambda: ACT(ssE[i][:], ssE[i][:], AF.Sqrt, [ssE[i], eps_t], [ssE[i]], scale=1.0 / D, bias=eps_t[:]),
                    lambda: P.op("dve", lambda e: e.reciprocal(ssE[i][:], ssE[i][:]), [ssE[i]], [ssE[i]]),
                    lambda: STT(fsb[i][:], fsb[i][:], ssE[i][:], mpost_b[:], ALU.mult, ALU.mult, [fsb[i], ssE[i], mpost_b], [fsb[i]]),
                    lambda: (TT(h1[i][:], h1[i][:], fsb[i][:], ALU.add, [h1[i], fsb[i]], [h1[i]]),
                             CP("act", pbb[:], xt2[:, 0:256], [xt2], [pbb])),
                    lambda: CP("act", hbb[:], h1[i][:], [h1[i]], [hbb]),
                    lambda: [TR(psbf(tb)[:, k * 128:(k + 1) * 128], hbb[:, k * 128:(k + 1) * 128], identb[:], [hbb, identb], [PS[tb]], k == 7)
                             for k in range(8)],
                    lambda: CP("dve", nT[i][:], psbf(tb).rearrange("p (k c) -> p k c", k=8), [PS[tb]], [nT[i]]),
                    lambda: [TR(psbf(tb)[:, k * 128:(k + 1) * 128], pbb[:, k * 128:(k + 1) * 128], identb[:], [pbb, identb], [PS[tb]], k == 1)
                             for k in range(2)],
                    lambda: CP("dve", pT[i][:], psbf(tb)[:, 0:256].rearrange("p (k c) -> p k c", k=2), [PS[tb]], [pT[i]]),
                ]

            def st_E1(u):
                ub = u % 2
                for half in range(2):
                    wo = next_w()
                    P.dma("sp", wo[:], wb_out[half], reads=[WB["out"][half]], writes=[wo])
                    for il in range(2):
                        t, i = u * 2 + il, ub * 2 + il
                        for j in range(8):
                            MM(PS[il][:, :], mix_all[:, j, t * 128:(t + 1) * 128], wo[:, j, :], j == 0, j == 7,
                               [mixG[t], mixM[t], wo], [PS[il]], j == 7)
                        CP("act", fsb[i][:, half * 512:(half + 1) * 512], PS[il][:, :], [PS[il]], [fsb[i]])

            def zipped(lists):
                out = []
                n_ = max(len(l) for l in lists)
                for ui in range(n_):
                    for l in lists:
                        if ui < len(l):
                            out.append(l[ui])
                return out

            def e1b_list(u):
                ub = u % 2
                return zipped([e1b_units(u * 2, ub * 2), e1b_units(u * 2 + 1, ub * 2 + 1)])

            def e4_list(u):
                ub = u % 2
                return zipped([e4_units(u * 2, ub * 2), e4_units(u * 2 + 1, ub * 2 + 1)])

            def st_E2(u, extra=()):
                extra = list(extra)
                ub = u % 2
                cs = slice(ub * 256, (ub + 1) * 256)
                wu = next_w()
                P.dma("sp", wu[:], wb_up[0], reads=[WB["up"][0]], writes=[wu])
                for cg in range(8):
                    wcur = wu
                    if cg < 7:
                        wu = next_w()
                        P.dma("sp", wu[:], wb_up[cg + 1], reads=[WB["up"][cg + 1]], writes=[wu])
                    for jj in range(4):
                        j = cg * 4 + jj
                        pb = 3 + j % 4
                        sqb = sqH[j % 4]
                        for k in range(8):
                            MM(PS[pb][:, 0:256], wcur[:, k, jj * 128:(jj + 1) * 128], nT_all[:, k, cs], k == 0, k == 7,
                               [wcur, nT[ub * 2], nT[ub * 2 + 1]], [PS[pb]], k == 7)
                        ACT(sqb[:], PS[pb][:, 0:256], AF.Square, [PS[pb]], [sqb])
                        STT(hidS[ub][j][:], PS[pb][:, 0:256], 0.0, sqb[:], ALU.is_gt, ALU.mult, [PS[pb], sqb], [hidS[ub][j]])
                        if extra:
                            extra.pop(0)()
                for f_ in extra:
                    f_()

            def st_E3(u, extra=()):
                extra = list(extra)
                ub = u % 2
                wd = wdn[0]
                P.dma("sp", wd[:], wb_down[0], reads=[WB["down"][0]], writes=[wd])
                for qq in range(4):
                    wcur = wd
                    if qq < 3:
                        wd = wdn[(qq + 1) % 2]
                        P.dma("sp", wd[:], wb_down[qq + 1], reads=[WB["down"][qq + 1]], writes=[wd])
                    for il in range(2):
                        i = ub * 2 + il
                        pb = 5 + il
                        for j in range(32):
                            MM(PS[pb][:, 0:256], hid_all[:, j, i * 128:(i + 1) * 128], wcur[:, j, :], j == 0, j == 31,
                               [hidS[ub][j], wcur], [PS[pb]], j == 31)
                        CP("act", fsb[i][:, qq * 256:(qq + 1) * 256], PS[pb][:, 0:256], [PS[pb]], [fsb[i]])
                        for _ in range(3):
                            if extra:
                                extra.pop(0)()
                for f_ in extra:
                    f_()

            def st_E45(u):
                ub = u % 2
                wgts = []
                for half in range(2):
                    wgt = next_w()
                    P.dma("sp", wgt[:], wb_gate[half], reads=[WB["gate"][half]], writes=[wgt])
                    wgts.append(wgt)
                wpl = next_w()
                wplv = wpl[:].rearrange("p k c -> p (k c)")[:, 0:2048].rearrange("p (k c) -> p k c", k=2)
                P.dma("sp", wplv, wb_ple, reads=WB["ple"], writes=[wpl])
                for half in range(2):
                    wgt = wgts[half]
                    hs = slice(half * 512, (half + 1) * 512)
                    for il in range(2):
                        t, i = u * 2 + il, ub * 2 + il
                        for k in range(8):
                            MM(PS[il][:, :], nT[i][:, k, :], wgt[:, k, :], k == 0, k == 7, [nT[i], wgt], [PS[il]], k == 7)
                        for k in range(2):
                            MM(PS[5 + il][:, :], pT[i][:, k, :], wplv[:, k, hs], k == 0, k == 1, [pT[i], wpl], [PS[5 + il]], k == 1)
                        ACT(fsb[i][:, hs], PS[il][:, :], AF.Sigmoid, [PS[il]], [fsb[i]])
                        TT(fsb[i][:, hs], fsb[i][:, hs], PS[5 + il][:, :], ALU.mult, [fsb[i], PS[5 + il]], [fsb[i]])
                        if half == 1:
                            TT(h1[i][:], h1[i][:], fsb[i][:], ALU.add, [fsb[i], h1[i]], [h1[i]], eng="pool")
                            ob_ = Buf(out_d[seq, t * 128:(t + 1) * 128, :], "out")
                            P.dma("pool", out_d[seq, t * 128:(t + 1) * 128, :], h1[i][:], reads=[h1[i]], writes=[ob_])
                            outbufs.append(ob_)

            NSUB = T // 256
            st_E1(0)
            for f_ in e1b_list(0):
                f_()
            st_E2(0)
            st_E1(1)
            for u in range(NSUB):
                st_E3(u, e1b_list(u + 1) if u + 1 < NSUB else ())
                if u + 1 < NSUB:
                    st_E2(u + 1, e4_list(u))
                else:
                    for f_ in e4_list(u):
                        f_()
                st_E45(u)
                if u + 2 < NSUB:
                    st_E1(u + 2)

        if stop_after is None:
            P.wait_all("sp", outbufs + list(tap_out.values()))
        else:
            P.wait_all("sp", list(tap_out.values()))
        P.emit()
    nc._mk_stats = (P.nops, P.nwaits)
    return nc


_NC_CACHE = {}


def kernel(x, p, w_in, conv_w, a_log, dt_bias, gdn_norm_w, w_out, attn_pre_norm,
           attn_post_norm, mlp_pre_norm, mlp_post_norm, w_up, w_down, w_ple, w_ple_gate):
    n_cores = 8
    nseq = 2
    if "nc" not in _NC_CACHE:
        _NC_CACHE["nc"] = build(nseq=nseq)
    nc = _NC_CACHE["nc"]
    f = lambda a: np.ascontiguousarray(np.asarray(a, dtype=np.float32))
    shared = {
        "w_in": f(w_in)[0], "conv_w": f(conv_w)[0], "a_log": f(a_log)[0], "dt_bias": f(dt_bias)[0],
        "gdn_norm_w": f(gdn_norm_w)[0], "w_out": f(w_out)[0], "attn_pre_norm": f(attn_pre_norm)[0],
        "attn_post_norm": f(attn_post_norm)[0], "mlp_pre_norm": f(mlp_pre_norm)[0],
        "mlp_post_norm": f(mlp_post_norm)[0], "w_up": f(w_up)[0], "w_down": f(w_down)[0],
        "w_ple": f(w_ple)[0], "w_ple_gate": f(w_ple_gate)[0], "consts": make_consts(),
    }
    x = f(x)
    p = f(p)[0]
    in_maps = []
    for c in range(n_cores):
        m = dict(shared)
        m["x"] = np.ascontiguousarray(x[c * nseq:(c + 1) * nseq])
        m["p"] = np.ascontiguousarray(p[c * nseq:(c + 1) * nseq])
        in_maps.append(m)
    res = run_bass_kernel_spmd(nc, in_maps, core_ids=list(range(n_cores)))
    out = np.concatenate([np.asarray(r["out"], dtype=np.float32) for r in res.results], axis=0)
    return out
```
